# Optimizing a Trainium2 kernel written in Bass

```python
import jax
import jax.numpy as jnp
from jax import lax
import numpy as np

D_MODEL = 2048
BATCH = 4
SEQ = 4096
DEPTH = 1

CONV_WIDTH = D_MODEL // 2
CONV_KERNEL = 31
ATTN_WIDTH = D_MODEL - CONV_WIDTH
N_HEADS = 16
HEAD_DIM = ATTN_WIDTH // N_HEADS
MIX_WIDTH = CONV_WIDTH + ATTN_WIDTH
DILATED_PATTERNS = ((128, 1), (512, 4), (2048, 16))
SPLITS = (CONV_WIDTH, CONV_WIDTH, CONV_WIDTH, ATTN_WIDTH, ATTN_WIDTH, ATTN_WIDTH, ATTN_WIDTH)
IN_WIDTH = sum(SPLITS)
SPLIT_POINTS = tuple(int(i) for i in np.cumsum(SPLITS)[:-1])
RMS_EPS = 1e-6
LN_EPS = 1e-5
MASK_VALUE = -1e30

kernel_name = "hybrid_conformer_conv_dilated_attn"


def rms_norm(x, g):
    xf = x.astype(jnp.float32)
    y = xf * lax.rsqrt(jnp.mean(xf * xf, axis=-1, keepdims=True) + RMS_EPS)
    return (y * g.astype(jnp.float32)).astype(x.dtype)


def layer_norm(x, g, b):
    xf = x.astype(jnp.float32)
    mu = jnp.mean(xf, axis=-1, keepdims=True)
    var = jnp.mean(jnp.square(xf - mu), axis=-1, keepdims=True)
    y = (xf - mu) * lax.rsqrt(var + LN_EPS)
    return (y * g.astype(jnp.float32) + b.astype(jnp.float32)).astype(x.dtype)


def causal_depthwise_conv(u, w, b):
    k, c = w.shape
    y = lax.conv_general_dilated(
        u, w[:, None, :].astype(u.dtype), window_strides=(1,),
        padding=[(k - 1, 0)], dimension_numbers=("NWC", "WIO", "NWC"),
        feature_group_count=c)
    return y + b.astype(u.dtype)


def dilated_window_attention(q, k, v, window, dilation):
    b, s, h, dh = q.shape
    nb = window // dilation
    sub_len = -(-s // dilation)
    n_blk = -(-sub_len // nb)
    s_pad = n_blk * nb * dilation
    pad = ((0, 0), (0, s_pad - s), (0, 0), (0, 0))

    def to_blocks(t):
        return jnp.pad(t, pad).reshape(b, n_blk, nb, dilation, h, dh)

    def with_prev(t):
        prev = jnp.concatenate([jnp.zeros_like(t[:, :1]), t[:, :-1]], axis=1)
        return jnp.concatenate([prev, t], axis=2)

    qb = to_blocks(q)
    kb = with_prev(to_blocks(k))
    vb = with_prev(to_blocks(v))
    scores = jnp.einsum("bnqrhd,bnkrhd->bnrhqk", qb, kb,
                        preferred_element_type=jnp.float32) * (dh ** -0.5)
    qi = jnp.arange(nb)[:, None]
    ki = jnp.arange(2 * nb)[None, :]
    delta = nb + qi - ki
    band = (delta >= 0) & (delta <= nb)
    valid = band[None] & ((jnp.arange(n_blk)[:, None, None] > 0) | (ki[None] >= nb))
    scores = jnp.where(valid[None, :, None, None], scores, MASK_VALUE)
    m = jnp.max(scores, axis=-1, keepdims=True)
    p = jnp.exp(scores - m)
    l = jnp.sum(p, axis=-1)
    o = jnp.einsum("bnrhqk,bnkrhd->bnqrhd", p.astype(v.dtype), vb,
                   preferred_element_type=jnp.float32)
    o = o / jnp.transpose(l, (0, 1, 4, 2, 3))[..., None]
    lse = jnp.transpose(m[..., 0] + jnp.log(l), (0, 1, 4, 2, 3))
    o = o.reshape(b, s_pad, h, dh)[:, :s]
    lse = lse.reshape(b, s_pad, h)[:, :s]
    return o, lse


def dilated_attention_mixture(q, k, v):
    outs, lses = [], []
    for window, dilation in DILATED_PATTERNS:
        o, lse = dilated_window_attention(q, k, v, window, dilation)
        outs.append(o)
        lses.append(lse)
    w = jax.nn.softmax(jnp.stack(lses, axis=0), axis=0)
    return jnp.einsum("pbsh,pbshd->bshd", w, jnp.stack(outs, axis=0))


def hybrid_layer(x, norm_g, w_in, conv_w, conv_b, conv_norm_g, conv_norm_b,
                 conv_pw_w, conv_pw_b, q_norm_g, k_norm_g, w_out):
    bsz, s, _ = x.shape
    h = rms_norm(x, norm_g)
    z = h @ w_in
    a_val, a_glu, a_gate, q, k, v, b_gate = jnp.split(z, SPLIT_POINTS, axis=-1)

    u = a_val * jax.nn.sigmoid(a_glu)
    u = causal_depthwise_conv(u, conv_w, conv_b)
    u = jax.nn.silu(layer_norm(u, conv_norm_g, conv_norm_b))
    u = u @ conv_pw_w + conv_pw_b
    y_a = u * jax.nn.silu(a_gate)

    q = rms_norm(q.reshape(bsz, s, N_HEADS, HEAD_DIM), q_norm_g)
    k = rms_norm(k.reshape(bsz, s, N_HEADS, HEAD_DIM), k_norm_g)
    v = v.reshape(bsz, s, N_HEADS, HEAD_DIM)
    o = dilated_attention_mixture(q, k, v).astype(x.dtype)
    y_b = o.reshape(bsz, s, ATTN_WIDTH) * jax.nn.silu(b_gate)

    y = jnp.concatenate([y_a, y_b], axis=-1) @ w_out
    return x + y


def setup_inputs(seed: int = 0) -> dict:
    key = jax.random.key(seed)
    ks = jax.random.split(key, 13)
    f32 = jnp.float32
    nrm = lambda k, shape, scale: jax.random.normal(k, shape, f32) * scale
    return {
        "x": nrm(ks[0], (BATCH, SEQ, D_MODEL), 1.0),
        "norm_g": 1.0 + nrm(ks[1], (DEPTH, D_MODEL), 0.02),
        "w_in": nrm(ks[2], (DEPTH, D_MODEL, IN_WIDTH), D_MODEL ** -0.5),
        "conv_w": nrm(ks[3], (DEPTH, CONV_KERNEL, CONV_WIDTH), CONV_KERNEL ** -0.5),
        "conv_b": nrm(ks[4], (DEPTH, CONV_WIDTH), 0.01),
        "conv_norm_g": 1.0 + nrm(ks[5], (DEPTH, CONV_WIDTH), 0.02),
        "conv_norm_b": nrm(ks[6], (DEPTH, CONV_WIDTH), 0.01),
        "conv_pw_w": nrm(ks[7], (DEPTH, CONV_WIDTH, CONV_WIDTH), CONV_WIDTH ** -0.5),
        "conv_pw_b": nrm(ks[8], (DEPTH, CONV_WIDTH), 0.01),
        "q_norm_g": 1.0 + nrm(ks[9], (DEPTH, HEAD_DIM), 0.02),
        "k_norm_g": 1.0 + nrm(ks[10], (DEPTH, HEAD_DIM), 0.02),
        "w_out": nrm(ks[11], (DEPTH, MIX_WIDTH, D_MODEL), MIX_WIDTH ** -0.5),
    }


def reference(x, norm_g, w_in, conv_w, conv_b, conv_norm_g, conv_norm_b,
              conv_pw_w, conv_pw_b, q_norm_g, k_norm_g, w_out):
    for i in range(DEPTH):
        x = hybrid_layer(x, norm_g[i], w_in[i], conv_w[i], conv_b[i],
                         conv_norm_g[i], conv_norm_b[i], conv_pw_w[i], conv_pw_b[i],
                         q_norm_g[i], k_norm_g[i], w_out[i])
    return x
```

```python
import numpy as np
import ml_dtypes
from contextlib import ExitStack
import concourse.bass as bass
import concourse.mybir as mybir
from concourse.bass_utils import run_bass_kernel_spmd

F32 = mybir.dt.float32
BF16 = mybir.dt.bfloat16
ALU = mybir.AluOpType
AF = mybir.ActivationFunctionType
COMPUTE = ("pe", "act", "dve", "pool")

D = 2048
S = 4096
NB = 4
TPC = 2048
TB = 1024
IN_W = 7168
COL = dict(val=0, glu=1024, gate=2048, q=3072, k=4096, v=5120, bg=6144)
RMS_EPS = 1e-6
LN_EPS = 1e-5
SB_BASE = 16512
SB_END = 229344
NPE = 16


def sl(start, count, step=1):
    return slice(start, start + (count - 1) * step + 1, step)


class Reg:
    __slots__ = ("name", "w", "r", "arena")

    def __init__(self, name, arena=False):
        self.name = name
        self.w = None
        self.r = []
        self.arena = arena


class Op:
    __slots__ = ("eng", "fn", "deps", "is_dma", "signal", "ticket", "sem", "idx")

    def __init__(self, eng, fn, is_dma):
        self.eng = eng
        self.fn = fn
        self.is_dma = is_dma
        self.deps = []
        self.signal = False
        self.ticket = None
        self.sem = None


class Prog:
    def __init__(self, nc):
        self.nc = nc
        self.ops = []
        self.n_dma_sems = {"sp": 24, "pool": 6, "act": 2}
        self.last_arena = {}
        self.arena_dmas = []
        self.pending = {}

    def fence(self):
        f = list(self.last_arena.values()) + self.arena_dmas
        self.arena_dmas = []
        for e in ("pe", "act", "dve", "pool", "sp"):
            old = [d for d in self.pending.get(e, []) if d.is_dma]
            self.pending[e] = old + f

    def fence_regs(self, regs):
        f = {}
        for r in regs:
            if r.w is not None:
                f[r.w.idx] = r.w
            for rd in r.r:
                f[rd.idx] = rd
        f = list(f.values())
        for e in ("pe", "act", "dve", "pool", "sp"):
            old = [d for d in self.pending.get(e, []) if d.is_dma]
            self.pending[e] = old + f

    def add(self, eng, fn, R=(), W=(), is_dma=False):
        op = Op(eng, fn, is_dma)
        op.idx = len(self.ops)
        deps = {}
        arena = False
        for r in R:
            arena |= r.arena
            if r.w is not None:
                deps[r.w.idx] = r.w
        for w in W:
            arena |= w.arena
            if w.w is not None:
                deps[w.w.idx] = w.w
            for rd in w.r:
                deps[rd.idx] = rd
        if arena:
            for d in self.pending.get(eng, ()):
                deps[d.idx] = d
            self.pending[eng] = []
        best = {}
        for d in deps.values():
            if d.is_dma:
                best[("dma", d.idx)] = d
            else:
                if (not is_dma) and d.eng == eng and eng == "pe":
                    continue
                b = best.get(d.eng)
                if b is None or d.idx > b.idx:
                    best[d.eng] = d
        for d in best.values():
            d.signal = True
            op.deps.append(d)
        for w in W:
            w.w = op
            w.r = []
        for r in R:
            if r.w is not op:
                if is_dma:
                    r.r.append(op)
                else:
                    r.r = [x for x in r.r if x.is_dma or x.eng != eng]
                    r.r.append(op)
        if arena:
            if is_dma:
                self.arena_dmas.append(op)
            else:
                self.last_arena[eng] = op
        self.ops.append(op)
        return op

    def pe(self, fn, R=(), W=()):
        return self.add("pe", fn, R, W)

    def act(self, fn, R=(), W=()):
        return self.add("act", fn, R, W)

    def dve(self, fn, R=(), W=()):
        return self.add("dve", fn, R, W)

    def pool(self, fn, R=(), W=()):
        return self.add("pool", fn, R, W)

    def dma(self, queue, out, in_, R=(), W=(), **kw):
        return self.add(queue, lambda e: e.dma_start(out=out, in_=in_, **kw), R, W, is_dma=True)

    def emit(self, final_wait_ops=()):
        nc = self.nc
        with ExitStack() as es:
            esem = {e: es.enter_context(nc.semaphore("s_" + e)) for e in COMPUTE}
            dsem = {q: [es.enter_context(nc.semaphore(f"d_{q}{i}")) for i in range(n)]
                    for q, n in self.n_dma_sems.items()}
            ecount = {e: 0 for e in COMPUTE}
            dcount = {q: [0] * n for q, n in self.n_dma_sems.items()}
            drr = {q: 0 for q in self.n_dma_sems}
            dprev = {}
            for op in self.ops:
                if op.is_dma:
                    q = op.eng
                    j = drr[q] % self.n_dma_sems[q]
                    drr[q] += 1
                    op.sem = dsem[q][j]
                    dprev[op.idx] = dcount[q][j]
                    dcount[q][j] += 16
                    op.ticket = dcount[q][j]
                elif op.signal:
                    ecount[op.eng] += 1
                    op.ticket = ecount[op.eng]
                    op.sem = esem[op.eng]
            by_eng = {}
            for op in self.ops:
                by_eng.setdefault(op.eng, []).append(op)
            final_wait_ops = list(final_wait_ops)

            def run(engname, e):
                waited = {}

                def wait(sem, val):
                    k = id(sem)
                    if waited.get(k, 0) >= val:
                        return
                    waited[k] = val
                    e.wait_ge(sem, val)

                for op in by_eng.get(engname, []):
                    for d in op.deps:
                        wait(d.sem, d.ticket)
                    if op.is_dma:
                        if dprev[op.idx] > 0:
                            wait(op.sem, dprev[op.idx])
                        op.fn(e).then_inc(op.sem, 16)
                    else:
                        ins = op.fn(e)
                        if op.signal:
                            ins.then_inc(op.sem, 1)
                if engname == "sp":
                    for op in final_wait_ops:
                        wait(op.sem, op.ticket)

            with nc.Block() as block:
                @block.sync
                def _(e):
                    run("sp", e)

                @block.tensor
                def _(e):
                    run("pe", e)

                @block.scalar
                def _(e):
                    run("act", e)

                @block.vector
                def _(e):
                    run("dve", e)

                @block.gpsimd
                def _(e):
                    run("pool", e)


class Alloc:
    def __init__(self, nc):
        self.nc = nc
        self.pp = SB_BASE
        self.abase = None
        self.ap_ = None
        self.ctr = 0
        self.peak = 0

    def _mk(self, name, shape, dt, off):
        self.ctr += 1
        return self.nc.alloc_sbuf_tensor_at(f"{name}_{self.ctr}", list(shape), dt, offset=off)

    @staticmethod
    def _bytes(shape, dt):
        n = 1
        for s in shape[1:]:
            n *= s
        n *= 2 if dt == BF16 else 4
        return (n + 31) // 32 * 32

    def persist(self, name, shape, dt):
        assert self.abase is None
        t = self._mk(name, shape, dt, self.pp)
        self.pp += self._bytes(shape, dt)
        return t

    def start_arena(self):
        self.abase = self.pp
        self.ap_ = self.pp

    def reset(self, to=None):
        self.ap_ = self.abase if to is None else to

    def mark(self):
        return self.ap_

    def arena(self, name, shape, dt, reg=True):
        t = self._mk(name, shape, dt, self.ap_)
        self.ap_ += self._bytes(shape, dt)
        self.peak = max(self.peak, self.ap_)
        assert self.ap_ <= SB_END, (name, self.ap_, SB_END)
        return (t, Reg(name, arena=True)) if reg else t


def build_program():
    nc = bass.Bass("TRN2", target_bir_lowering=False)
    dt_in = lambda name, shape, dt=F32: nc.dram_tensor(name, list(shape), dt, kind="ExternalInput").ap()
    xe = dt_in("xe", [2 * TPC, D])
    gb = dt_in("gb", [128, D])
    w_in = dt_in("w_in", [D, IN_W])
    convw_d = dt_in("convw", [128, 8, 31])
    vec_d = dt_in("vec", [128, 4, 8])
    qkg_d = dt_in("qkg", [128, 2])
    pw_d = dt_in("pw", [1024, 1024])
    w_out = dt_in("w_out", [D, D])
    flag_d = dt_in("flag", [128, 1])
    mask128_d = dt_in("mask128", [128, 512], BF16)
    mask16_d = dt_in("mask16", [128, 512], BF16)
    ident_d = dt_in("ident", [128, 128], BF16)
    onesbd_d = dt_in("onesbd", [128, 128], BF16)
    onesf_d = dt_in("onesf", [128, 128], BF16)
    out_d = nc.dram_tensor("out", [TPC, D], F32, kind="ExternalOutput").ap()
    kT_s = nc.dram_tensor("kT_s", [8, 128, 2 * TPC], BF16, kind="Internal").ap()
    v_s = nc.dram_tensor("v_s", [2 * TPC, 16, 65], BF16, kind="Internal").ap()
    d_s = nc.dram_tensor("d_s", [4, 1, TB], F32, kind="Internal").ap()

    P = Prog(nc)
    A = Alloc(nc)

    hT = A.persist("hT", [128, 16, TB], BF16)
    hT_r = [Reg(f"hT{t}") for t in range(8)]
    hT_off = SB_BASE
    wbuf, wbuf_pw, wreg = [], [], []
    for i in range(3):
        off = A.pp
        wbuf.append(A.persist(f"wbuf{i}", [128, 16, 512], BF16))
        wbuf_pw.append(A._mk(f"wbufpw{i}", [128, 8, 1024], BF16, off))
        wreg.append(Reg(f"wbuf{i}"))
    mix_off = A.pp
    mixT = A.persist("mixT", [128, 16, TB], BF16)
    mix_r = [Reg(f"mix{c}") for c in range(16)]
    hTB = A._mk("hTB", [128, 16, TB], BF16, mix_off)
    hTB_r = [Reg(f"hTB{t}", arena=True) for t in range(8)]
    H = dict(t=hT, r=hT_r)
    ident = A.persist("ident", [128, 128], BF16)
    onesbd = A.persist("onesbd", [128, 128], BF16)
    onesf = A.persist("onesf", [128, 128], BF16)
    mask128 = A.persist("mask128", [128, 512], BF16)
    mask16 = A.persist("mask16", [128, 512], BF16)
    convw = A.persist("convw", [128, 8, 31], F32)
    vec = A.persist("vec", [128, 4, 8], F32)
    vec05 = A.persist("vec05", [128, 2, 8], F32)
    qkg = A.persist("qkg", [128, 2], F32)
    flag = A.persist("flag", [128, 1], F32)
    u_tail = A.persist("u_tail", [128, 8, 30], F32)
    utail_r = [Reg(f"utail{c}") for c in range(8)]
    const_r = Reg("consts")
    A.start_arena()

    PP = [nc.alloc_psum_tensor(f"PP{i}", [128, 1024], F32) for i in range(4)]
    PPb = [t.bitcast(BF16) for t in PP]
    preg = [Reg(f"bank{i}") for i in range(8)]

    def bank_ap(b):
        return PP[b // 2][:, (b % 2) * 512:(b % 2) * 512 + 512]

    def bank_bf(b):
        return PPb[b // 2][:, (b % 2) * 1024:(b % 2) * 1024 + 1024].rearrange("p (c t) -> p c t", c=8)

    ctr = dict(pair=0, bank=0, w=0, sb=0, ob=0, ds=0)

    PS = dict(split=False)

    def next_pair():
        ctr["pair"] += 1
        return ctr["pair"] % (3 if PS["split"] else 4)

    def next_bank():
        ctr["bank"] += 1
        return ctr["bank"] % (6 if PS["split"] else 8)

    def s0_bank():
        ctr["s0b"] = ctr.get("s0b", 0) + 1
        return 6 + ctr["s0b"] % 2

    def load_w(src, pw=False):
        i = ctr["w"] % 3
        ctr["w"] += 1
        dst = wbuf_pw[i] if pw else wbuf[i]
        P.dma("pool", dst[:], src.rearrange("(c p) n -> p c n", p=128), W=[wreg[i]])
        return dst, wreg[i]

    cregs = []
    ident_r = Reg("ident")
    P.dma("sp", ident[:], ident_d, W=[ident_r])
    rest_consts = ((onesbd, onesbd_d), (onesf, onesf_d), (mask128, mask128_d),
                   (mask16, mask16_d), (convw, convw_d), (vec, vec_d), (qkg, qkg_d), (flag, flag_d))

    def load_consts():
        for dst, src in rest_consts:
            cregs.append(Reg("c%d" % len(cregs)))
            P.dma("sp", dst[:], src, W=[cregs[-1]])

    def finish_consts():
        P.dve(lambda e: e.tensor_scalar(out=vec05[:], in0=vec[:, 1:3, :], scalar1=0.5, scalar2=None, op0=ALU.mult),
              R=cregs + [ident_r], W=[const_r])

    kTs_r = [[Reg(f"kTs{hp}_{b}") for b in range(NB)] for hp in range(8)]
    vs_r = [Reg(f"vs{b}") for b in range(NB)]
    ds_r = [Reg(f"ds{i}") for i in range(4)]
    out_ops = []

    def s0_alloc():
        return dict(
            xt=[A.arena(f"xt{i}", [128, D], F32) for i in range(3)],
            xn=[A.arena(f"xn{i}", [128, D], BF16) for i in range(2)],
            gbt=A.arena("gbt", [128, D], F32),
            ss=[A.arena(f"ss{i}", [128, 1], F32) for i in range(2)],
            rs=[A.arena(f"rs{i}", [128, 1], F32) for i in range(2)])

    def s0_gen(blk, hT, hT_r, bufs, deep=False):
        xt, xn, ss, rs = bufs["xt"], bufs["xn"], bufs["ss"], bufs["rs"]
        gbt, gbt_r = bufs["gbt"]
        P.dma("sp", gbt[:], gb, W=[gbt_r])
        banks = {}

        def stL(tt):
            (x_t, x_r) = xt[tt % 3]
            row0 = blk * TB + tt * 128
            P.dma("sp", x_t[:], xe[row0:row0 + 128, :], W=[x_r])

        def stA(tt):
            i = tt % 2
            (x_t, x_r), (s_t, s_r), (r_t, r_r) = xt[tt % 3], ss[i], rs[i]
            (j_t, j_r) = xn[i]
            P.act(lambda e: e.activation(out=j_t[:], in_=x_t[:], func=AF.Square, accum_out=s_t[:]),
                  R=[x_r], W=[j_r, s_r])
            P.act(lambda e: e.activation(out=r_t[:], in_=s_t[:], func=AF.Ln, bias=RMS_EPS, scale=1.0 / D),
                  R=[s_r], W=[r_r])
            P.act(lambda e: e.activation(out=r_t[:], in_=r_t[:], func=AF.Exp, scale=-0.5), R=[r_r], W=[r_r])

        def stB(tt):
            i = tt % 2
            (x_t, x_r), (n_t, n_r), (r_t, r_r) = xt[tt % 3], xn[i], rs[i]
            P.dve(lambda e: e.scalar_tensor_tensor(
                out=n_t[:], in0=x_t[:], scalar=r_t[:], in1=gbt[:], op0=ALU.mult, op1=ALU.mult),
                R=[x_r, r_r, gbt_r], W=[n_r])
            banks[tt] = []
            for half in range(2):
                b = s0_bank()
                banks[tt].append(b)
                for c8 in range(8):
                    c = half * 8 + c8
                    P.pe(lambda e, b=b, c8=c8, c=c: e.transpose(
                        out=bank_bf(b)[:, c8, :], in_=n_t[:, c * 128:(c + 1) * 128], identity=ident[:]),
                        R=[n_r, ident_r], W=[preg[b]])

        def stC(tt):
            for half in range(2):
                b = banks[tt][half]
                dst = hT[:, half * 8:half * 8 + 8, tt * 128:(tt + 1) * 128]
                if half == 0:
                    P.act(lambda e, b=b, dst=dst: e.copy(out=dst, in_=bank_bf(b)), R=[preg[b]], W=[hT_r[tt]])
                else:
                    P.dve(lambda e, b=b, dst=dst: e.tensor_copy(out=dst, in_=bank_bf(b)), R=[preg[b]], W=[hT_r[tt]])

        stL(0)
        if deep:
            stL(1)
        for it in range(8 + 1):
            if not deep and it + 1 < 8:
                stL(it + 1)
            if it < 8:
                stA(it)
            if 0 <= it - 1 < 8:
                stB(it - 1)
                stC(it - 1)
            if deep and it + 2 < 8:
                stL(it + 2)
            yield

    def run_all(gen):
        for _ in gen:
            pass

    def proj_fm(wb, w_r, j, p=None):
        if p is None:
            p = next_pair()
        hT, hT_r = H["t"], H["r"]
        for half in range(2):
            for c in range(16):
                P.pe(lambda e, p=p, half=half, c=c: e.matmul(
                    PP[p][:, half * 512:half * 512 + 512], lhsT=wb[:, c, j * 128:(j + 1) * 128],
                    rhs=hT[:, c, half * 512:half * 512 + 512], start=(c == 0), stop=(c == 15)),
                    R=[w_r] + hT_r[4 * half:4 * half + 4], W=[preg[2 * p + half]])
        return p

    def pr(p):
        return [preg[2 * p], preg[2 * p + 1]]

    def head_norm(p, gcol, tmp, dst, dst_regs, p2=None):
        (sq_t, sq_r), (ln_t, ln_r) = tmp
        P.act(lambda e: e.activation(out=sq_t[:], in_=PP[p][:], func=AF.Square), R=pr(p), W=[sq_r])
        if p2 is None:
            p2 = next_pair()
        for half in range(2):
            P.pe(lambda e, half=half: e.matmul(PP[p2][:, half * 512:half * 512 + 512], lhsT=onesbd[:],
                                               rhs=sq_t[:, half * 512:half * 512 + 512], start=True, stop=True),
                 R=[sq_r, const_r], W=[preg[2 * p2 + half]])
        P.act(lambda e: e.activation(out=ln_t[:], in_=PP[p2][:], func=AF.Ln, bias=RMS_EPS, scale=1.0 / 64),
              R=pr(p2), W=[ln_r])
        P.act(lambda e: e.activation(out=ln_t[:], in_=ln_t[:], func=AF.Exp, scale=-0.5), R=[ln_r], W=[ln_r])
        P.dve(lambda e: e.scalar_tensor_tensor(out=dst, in0=PP[p][:], scalar=qkg[:, gcol:gcol + 1], in1=ln_t[:],
                                               op0=ALU.mult, op1=ALU.mult),
              R=pr(p) + [ln_r, const_r], W=dst_regs)

    def kv_alloc():
        return dict(vnat=A.arena("vnat", [128, 8, 16, 65], BF16),
                    vregs=[Reg(f"vnat{t}", arena=True) for t in range(8)],
                    tmps=[(A.arena(f"ksq{i}", [128, TB], BF16), A.arena(f"kln{i}", [128, TB], F32)) for i in range(2)],
                    kTo=[A.arena(f"kTo{i}", [128, TB], BF16) for i in range(2)])

    def stage_kv(blk, bufs, bg=None):
        vnat, vnat_r = bufs["vnat"]
        vregs = bufs["vregs"]
        tmps, kTo = bufs["tmps"], bufs["kTo"]
        hT, hT_r = H["t"], H["r"]
        P.dve(lambda e: e.memset(vnat[:, :, :, 64:65], 1.0), W=vregs)
        if blk < 2:
            P.dve(lambda e: e.tensor_scalar(out=vnat[:, :, :, 64:65], in0=vnat[:, :, :, 64:65], scalar1=flag[:, 0:1],
                                            scalar2=None, op0=ALU.mult), R=[const_r], W=vregs)
        for cb in range(2):
            wb, w_r = load_w(w_in[:, COL["v"] + cb * 512:COL["v"] + cb * 512 + 512])
            for tt in range(8):
                b = next_bank()
                for c in range(16):
                    P.pe(lambda e, b=b, c=c, tt=tt, wb=wb: e.matmul(
                        bank_ap(b), lhsT=hT[:, c, tt * 128:(tt + 1) * 128], rhs=wb[:, c, :],
                        start=(c == 0), stop=(c == 15)), R=[w_r, hT_r[tt]], W=[preg[b]])
                dst = vnat[:, tt, cb * 8:cb * 8 + 8, 0:64]
                src = lambda b=b: bank_ap(b).rearrange("p (h e) -> p h e", e=64)
                if tt % 2 == 0:
                    P.act(lambda e, dst=dst, src=src: e.copy(out=dst, in_=src()), R=[preg[b]], W=[vregs[tt]])
                else:
                    P.dve(lambda e, dst=dst, src=src: e.tensor_copy(out=dst, in_=src()), R=[preg[b]], W=[vregs[tt]])
                if bg is not None and tt % 4 == 3:
                    next(bg, None)
        P.dma("sp", v_s[blk * TB:(blk + 1) * TB].rearrange("(t p) h e -> p t (h e)", p=128),
              vnat[:].rearrange("p t h e -> p t (h e)"), R=vregs, W=[vs_r[blk]])
        wk = [load_w(w_in[:, COL["k"] + wi * 512:COL["k"] + wi * 512 + 512]) for wi in range(2)]
        p_next = proj_fm(wk[0][0], wk[0][1], 0)
        for hp in range(8):
            p = p_next
            pipelined = not PS["split"]
            if hp + 1 < 8 and pipelined:
                wb, w_r = wk[(hp + 1) // 4]
                p_next = proj_fm(wb, w_r, (hp + 1) % 4)
            ko, ko_r = kTo[hp % 2]
            head_norm(p, 1, tmps[hp % 2], ko[:], [ko_r])
            if hp + 1 < 8 and not pipelined:
                wb, w_r = wk[(hp + 1) // 4]
                p_next = proj_fm(wb, w_r, (hp + 1) % 4)
            P.dma("sp", kT_s[hp, :, blk * TB:(blk + 1) * TB], ko[:], R=[ko_r], W=[kTs_r[hp][blk]])
            if bg is not None:
                next(bg, None)

    def act_gate(pg, th, hg=None):
        (th_t, th_r) = th
        P.act(lambda e: e.activation(out=th_t[:], in_=PP[pg][:], func=AF.Tanh, scale=0.5), R=pr(pg), W=[th_r])
        if hg is not None:
            (hg_t, hg_r) = hg
            P.act(lambda e: e.activation(out=hg_t[:], in_=PP[pg][:], func=AF.Copy, scale=0.5), R=pr(pg), W=[hg_r])

    def ch_alloc():
        return dict(th=A.arena("th", [128, 128], F32), hv=A.arena("hv", [128, 128], F32))

    def stage_conv_halo(bufs, bgen=None):
        th, th_r = bufs["th"]
        hv, hv_r = bufs["hv"]
        hT, hT_r = H["t"], H["r"]
        for wi in range(2):
            wv, wv_r = load_w(w_in[:, COL["val"] + wi * 512:COL["val"] + wi * 512 + 512])
            wg, wg_r = load_w(w_in[:, COL["glu"] + wi * 512:COL["glu"] + wi * 512 + 512])
            for j in range(4):
                c = wi * 4 + j
                bv, bg = next_bank(), next_bank()
                for (b, wb, w_r) in ((bv, wv, wv_r), (bg, wg, wg_r)):
                    for ck in range(16):
                        P.pe(lambda e, b=b, wb=wb, ck=ck, j=j: e.matmul(
                            bank_ap(b)[:, 0:128], lhsT=wb[:, ck, j * 128:(j + 1) * 128], rhs=hT[:, ck, 896:1024],
                            start=(ck == 0), stop=(ck == 15)), R=[w_r, hT_r[7]], W=[preg[b]])
                P.act(lambda e, bg=bg: e.activation(out=th[:], in_=bank_ap(bg)[:, 0:128], func=AF.Tanh, scale=0.5),
                      R=[preg[bg]], W=[th_r])
                P.act(lambda e, bv=bv: e.activation(out=hv[:], in_=bank_ap(bv)[:, 0:128], func=AF.Copy, scale=0.5),
                      R=[preg[bv]], W=[hv_r])
                P.dve(lambda e, c=c: e.scalar_tensor_tensor(out=u_tail[:, c, :], in0=th[:, 98:128], scalar=1.0,
                                                            in1=hv[:, 98:128], op0=ALU.add, op1=ALU.mult),
                      R=[th_r, hv_r], W=[utail_r[c]])
                if bgen is not None and c % 2 == 1:
                    next(bgen, None)

    def stage_conv(blk):
        A.reset()
        y = A.arena("y", [128, 8, TB], F32, reg=False)
        y_r = [Reg(f"y{c}", arena=True) for c in range(8)]
        mk = A.mark()
        u = [A.arena(f"u{i}", [128, TB + 30], F32) for i in range(2)]
        ubf = [A.arena(f"ubf{i}", [128, TB + 32], BF16) for i in range(2)]
        th = [A.arena(f"th{i}", [128, TB], F32) for i in range(2)]
        hv = [A.arena(f"hv{i}", [128, TB], F32) for i in range(2)]
        accA = [A.arena(f"accA{i}", [128, TB], F32) for i in range(2)]
        wd = [A.arena(f"wd{i}", [128, max(NPE, 1), 128], BF16) for i in range(2)]
        P.fence()
        tD = list(range(0, 31 - NPE))
        tP = list(range(31 - NPE, 31))
        wts = {}

        def load_pair(wi):
            wts[wi] = (load_w(w_in[:, COL["val"] + wi * 512:COL["val"] + wi * 512 + 512]),
                       load_w(w_in[:, COL["glu"] + wi * 512:COL["glu"] + wi * 512 + 512]))

        def proj_fixed(wb, w_r, j, p):
            for half in range(2):
                for ck in range(16):
                    P.pe(lambda e, half=half, ck=ck: e.matmul(
                        PP[p][:, half * 512:half * 512 + 512], lhsT=wb[:, ck, j * 128:(j + 1) * 128],
                        rhs=hT[:, ck, half * 512:half * 512 + 512], start=(ck == 0), stop=(ck == 15)),
                        R=[w_r] + hT_r[4 * half:4 * half + 4], W=[preg[2 * p + half]])

        def proj(c):
            if c // 4 not in wts:
                load_pair(c // 4)
            (wv, wv_r), (wg, wg_r) = wts[c // 4]
            proj_fixed(wv, wv_r, c % 4, 0)
            proj_fixed(wg, wg_r, c % 4, 1)

        def merge_pe(c):
            pc = 2 + c % 2
            yc = y[:, c, :]
            P.dve(lambda e: e.tensor_tensor(out=yc, in0=PP[pc][:], in1=yc, op=ALU.add),
                  R=pr(pc) + [y_r[c]], W=[y_r[c]])

        def build_wd(c):
            wd_t, wd_r = wd[c % 2]
            for n_, k in enumerate(tP):
                P.act(lambda e, n_=n_, k=k: e.activation(
                    out=wd_t[:, n_, :], in_=ident[:], func=AF.Copy, scale=convw[:, c, k:k + 1]),
                    R=[const_r], W=[wd_r])

        proj(0)
        if NPE:
            build_wd(0)
        for c in range(8):
            i = c % 2
            pv, pg = 0, 1
            (u_t, u_r), (ub_t, ub_r), (th_t, th_r), (hv_t, hv_r), (aa_t, aa_r), (wd_t, wd_r) = \
                u[i], ubf[i], th[i], hv[i], accA[i], wd[i]
            act_gate(pg, th[i])
            P.act(lambda e, hv_t=hv_t: e.activation(out=hv_t[:], in_=PP[pv][:], func=AF.Copy, scale=0.5),
                  R=pr(pv), W=[hv_r])
            if c + 1 < 8:
                proj(c + 1)
            P.dve(lambda e, u_t=u_t, c=c: e.tensor_copy(out=u_t[:, 0:30], in_=u_tail[:, c, :]),
                  R=[utail_r[c]], W=[u_r])
            P.dve(lambda e, u_t=u_t, th_t=th_t, hv_t=hv_t: e.scalar_tensor_tensor(
                out=u_t[:, 30:30 + TB], in0=th_t[:], scalar=1.0, in1=hv_t[:], op0=ALU.add, op1=ALU.mult),
                R=[th_r, hv_r], W=[u_r])
            P.dve(lambda e, u_t=u_t, c=c: e.tensor_copy(out=u_tail[:, c, :], in_=u_t[:, TB:TB + 30]),
                  R=[u_r], W=[utail_r[c]])
            yc = y[:, c, :]
            if NPE:
                P.act(lambda e, u_t=u_t, ub_t=ub_t: e.copy(out=ub_t[:, 0:TB + 30], in_=u_t[:]), R=[u_r], W=[ub_r])
                if c + 1 < 8:
                    build_wd(c + 1)
                pc = 2 + c % 2
                for half in range(2):
                    for n_, k in enumerate(tP):
                        P.pe(lambda e, half=half, n_=n_, k=k, pc=pc, wd_t=wd_t, ub_t=ub_t: e.matmul(
                            PP[pc][:, half * 512:half * 512 + 512], lhsT=wd_t[:, n_, :],
                            rhs=ub_t[:, k + half * 512:k + half * 512 + 512], start=(n_ == 0), stop=(n_ == NPE - 1)),
                            R=[wd_r, ub_r], W=[preg[2 * pc + half]])
            tA = tD[0::2]
            tA2 = tD[1::2]
            P.dve(lambda e, u_t=u_t, c=c, yc=yc: e.tensor_scalar(
                out=yc, in0=u_t[:, 0:TB], scalar1=convw[:, c, 0:1], scalar2=vec[:, 0, c:c + 1],
                op0=ALU.mult, op1=ALU.add), R=[u_r, const_r], W=[y_r[c]])
            P.dve(lambda e, u_t=u_t, c=c, aa_t=aa_t: e.tensor_scalar(
                out=aa_t[:], in0=u_t[:, 1:1 + TB], scalar1=convw[:, c, 1:2], scalar2=None, op0=ALU.mult),
                R=[u_r, const_r], W=[aa_r])
            for k in tD[2:]:
                if k in tA:
                    P.dve(lambda e, u_t=u_t, c=c, k=k, yc=yc: e.scalar_tensor_tensor(
                        out=yc, in0=u_t[:, k:k + TB], scalar=convw[:, c, k:k + 1], in1=yc,
                        op0=ALU.mult, op1=ALU.add), R=[u_r, const_r, y_r[c]], W=[y_r[c]])
                else:
                    P.dve(lambda e, u_t=u_t, c=c, k=k, aa_t=aa_t: e.scalar_tensor_tensor(
                        out=aa_t[:], in0=u_t[:, k:k + TB], scalar=convw[:, c, k:k + 1], in1=aa_t[:],
                        op0=ALU.mult, op1=ALU.add), R=[u_r, const_r, aa_r], W=[aa_r])
            P.dve(lambda e, yc=yc, aa_t=aa_t: e.tensor_tensor(out=yc, in0=yc, in1=aa_t[:], op=ALU.add),
                  R=[y_r[c], aa_r], W=[y_r[c]])
            if NPE and c >= 1:
                merge_pe(c - 1)
        if NPE:
            merge_pe(7)
        A.reset(mk)
        ybf = [A.arena(f"ybf{i}", [128, TB], BF16) for i in range(2)]
        ysq = [A.arena(f"ysq{i}", [128, TB], BF16) for i in range(2)]
        mean, mean_r = A.arena("mean", [128, TB], F32)
        rstd, rstd_r = A.arena("rstd", [128, TB], F32)
        th2 = [A.arena(f"th2{i}", [128, TB], F32) for i in range(2)]
        vp = [A.arena(f"vp{i}", [128, TB], F32) for i in range(2)]
        tmp, tmp_r = vp[0]
        sga = A.arena("sga", [128, 8, TB], BF16, reg=False)
        sga_r = [Reg(f"sga{c}", arena=True) for c in range(8)]
        gth = A.arena("gth", [128, TB], F32)
        ghg = A.arena("ghg", [128, TB], F32)
        P.fence()
        wg0 = load_w(w_in[:, COL["gate"]:COL["gate"] + 512])
        wg1 = load_w(w_in[:, COL["gate"] + 512:COL["gate"] + 1024])
        wpw, wpw_r = load_w(pw_d, pw=True)
        pg_pre = {c_: proj_fm(wg0[0], wg0[1], c_) for c_ in range(2)}
        pS, pQ = next_pair(), next_pair()
        for c in range(8):
            i = c % 2
            (yb_t, yb_r), (ys_t, ys_r) = ybf[i], ysq[i]
            P.act(lambda e, c=c, yb_t=yb_t: e.copy(out=yb_t[:], in_=y[:, c, :]), R=[y_r[c]], W=[yb_r])
            P.act(lambda e, c=c, ys_t=ys_t: e.activation(out=ys_t[:], in_=y[:, c, :], func=AF.Square), R=[y_r[c]], W=[ys_r])
            for (p, t, r_) in ((pS, yb_t, yb_r), (pQ, ys_t, ys_r)):
                for half in range(2):
                    P.pe(lambda e, p=p, t=t, half=half, c=c: e.matmul(
                        PP[p][:, half * 512:half * 512 + 512], lhsT=onesf[:], rhs=t[:, half * 512:half * 512 + 512],
                        start=(c == 0), stop=(c == 7)), R=[r_, const_r], W=[preg[2 * p + half]])
        P.dve(lambda e: e.tensor_scalar(out=mean[:], in0=PP[pS][:], scalar1=1.0 / 1024, scalar2=None, op0=ALU.mult),
              R=pr(pS), W=[mean_r])
        P.dve(lambda e: e.tensor_tensor(out=tmp[:], in0=mean[:], in1=mean[:], op=ALU.mult), R=[mean_r], W=[tmp_r])
        P.dve(lambda e: e.scalar_tensor_tensor(out=tmp[:], in0=PP[pQ][:], scalar=1.0 / 1024, in1=tmp[:],
                                               op0=ALU.mult, op1=ALU.subtract), R=pr(pQ) + [tmp_r], W=[tmp_r])
        P.act(lambda e: e.activation(out=rstd[:], in_=tmp[:], func=AF.Ln, bias=LN_EPS), R=[tmp_r], W=[rstd_r])
        P.act(lambda e: e.activation(out=rstd[:], in_=rstd[:], func=AF.Exp, scale=-0.5), R=[rstd_r], W=[rstd_r])
        for c in range(8):
            i = c % 2
            (t2_t, t2_r), (vp_t, vp_r) = th2[i], vp[i]
            yc = y[:, c, :]
            wg, wg_r = (wg0, wg1)[c // 4]
            pg = pg_pre.pop(c) if c in pg_pre else proj_fm(wg, wg_r, c % 4)
            P.dve(lambda e, yc=yc: e.tensor_tensor(out=yc, in0=yc, in1=mean[:], op=ALU.subtract),
                  R=[y_r[c], mean_r], W=[y_r[c]])
            P.dve(lambda e, yc=yc: e.tensor_tensor(out=yc, in0=yc, in1=rstd[:], op=ALU.mult),
                  R=[y_r[c], rstd_r], W=[y_r[c]])
            P.act(lambda e, yc=yc, c=c, t2_t=t2_t: e.activation(out=t2_t[:], in_=yc, func=AF.Tanh,
                                                               scale=vec05[:, 0, c:c + 1], bias=vec05[:, 1, c:c + 1]),
                  R=[y_r[c], const_r], W=[t2_r])
            P.dve(lambda e, yc=yc, c=c, vp_t=vp_t: e.tensor_scalar(
                out=vp_t[:], in0=yc, scalar1=vec05[:, 0, c:c + 1], scalar2=vec05[:, 1, c:c + 1],
                op0=ALU.mult, op1=ALU.add), R=[y_r[c], const_r], W=[vp_r])
            P.dve(lambda e, c=c, t2_t=t2_t, vp_t=vp_t: e.scalar_tensor_tensor(
                out=mixT[:, 8 + c, :], in0=t2_t[:], scalar=1.0, in1=vp_t[:], op0=ALU.add, op1=ALU.mult),
                R=[t2_r, vp_r], W=[mix_r[8 + c]])
            act_gate(pg, gth, ghg)
            P.dve(lambda e, c=c: e.scalar_tensor_tensor(
                out=sga[:, c, :], in0=gth[0][:], scalar=1.0, in1=ghg[0][:], op0=ALU.add, op1=ALU.mult),
                R=[gth[1], ghg[1]], W=[sga_r[c]])
        for co in range(8):
            pp_ = next_pair()
            for half in range(2):
                for ci in range(8):
                    P.pe(lambda e, half=half, ci=ci, co=co, pp_=pp_: e.matmul(
                        PP[pp_][:, half * 512:half * 512 + 512], lhsT=wpw[:, ci, co * 128:(co + 1) * 128],
                        rhs=mixT[:, 8 + ci, half * 512:half * 512 + 512], start=(ci == 0), stop=(ci == 7)),
                        R=[wpw_r, mix_r[8 + ci]], W=[preg[2 * pp_ + half]])
            P.dve(lambda e, co=co, pp_=pp_: e.scalar_tensor_tensor(
                out=mixT[:, co, :], in0=PP[pp_][:], scalar=vec[:, 3, co:co + 1], in1=sga[:, co, :],
                op0=ALU.add, op1=ALU.mult), R=pr(pp_) + [sga_r[co], const_r], W=[mix_r[co]])

    runB = {}

    def stage_attn(blk):
        e0 = blk * TB
        A.reset()
        NVT = 53
        qTn = [A.arena(f"qTn{i}", [128, TB], BF16) for i in range(2)]
        q4b = [A.arena(f"q4b{i}", [128, 4, 256], BF16) for i in range(2)]
        q16b = [A.arena(f"q16b{i}", [128, 17, 64], BF16) for i in range(2)]
        sgbb = [A.arena(f"sgb{i}", [128, TB], BF16) for i in range(3)]
        runB["start"] = A.mark()
        qsq, qsq_r = A.arena("qsq", [128, 512], BF16)
        qraw, qraw_r = A.arena("qraw", [128, 512], F32)
        qln, qln_r = A.arena("qln", [128, 512], F32)
        bth, bth_r = A.arena("bth", [128, 512], F32)
        bhg, bhg_r = A.arena("bhg", [128, 512], F32)
        kTw = [A.arena(f"kTw{i}", [128, 3 * TB], BF16) for i in range(2)]
        vt = [A.arena(f"vt{i}", [128, NVT, 2, 65], BF16) for i in range(2)]
        pt = [A.arena(f"pt{i}", [128, 1024], BF16) for i in range(3)]
        runB["end"] = A.mark()
        acc = [A.arena(f"acc{i}", [65, TB], F32) for i in range(2)]
        dnos = [A.arena(f"dno{i}", [128, TB], F32) for i in range(2)]
        lo = e0 - 2 * TB
        vsR = [vs_r[b] for b in range(max(blk - 2, 0), blk + 1)]
        P.fence()
        BQ, BN = 6, 7

        vparts = [[Reg(f"vt{i}_{k}", arena=True) for k in range(7)] for i in range(2)]

        def load_kv(hp):
            i = hp % 2
            (k_t, k_r), (v_t, _v) = kTw[i], vt[i]
            vr = vparts[i]
            P.dma("sp", k_t[:], kT_s[hp, :, lo:e0 + TB], R=[kTs_r[hp][b] for b in range(blk - 2, blk + 1)], W=[k_r])
            hs = slice(2 * hp, 2 * hp + 2)
            P.dma("sp", v_t[:, 0:9], v_s[e0 - 128:e0 + TB, hs, :].rearrange("(t p) h e -> p t h e", p=128),
                  R=vsR, W=[vr[0]])
            for r in range(4):
                P.dma("sp", v_t[:, 9 + 3 * r:12 + 3 * r],
                      v_s[sl(e0 - 512 + r, 384, 4), hs, :].rearrange("(m p) h e -> p m h e", p=128), R=vsR, W=[vr[1 + r]])
            P.dma("sp", v_t[:, 21:37], v_s[lo:e0, hs, :].rearrange("(p r) h e -> p r h e", r=16), R=vsR, W=[vr[5]])
            P.dma("sp", v_t[:, 37:53], v_s[e0 - TB:e0 + TB, hs, :].rearrange("(p r) h e -> p r h e", r=16),
                  R=vsR, W=[vr[6]])

        wts = {}

        def get_w(kind, hp):
            key = (kind, hp // 4)
            if key not in wts:
                c0 = COL[kind] + (hp // 4) * 512
                wts[key] = load_w(w_in[:, c0:c0 + 512])
            return wts[key]

        def prep_gen(hp):
            i = hp % 2
            j = hp % 4
            (qn_t, qn_r), (q4_t, q4_r), (q16_t, q16_r), (sg_t, sg_r) = qTn[i], q4b[i], q16b[i], sgbb[hp % 3]

            half_box = [0]

            def mm4(kind, half, c0):
                wb, w_r = get_w(kind, hp)
                for c in range(c0, c0 + 4):
                    P.pe(lambda e, c=c: e.matmul(bank_ap(BQ), lhsT=wb[:, c, j * 128:(j + 1) * 128],
                                                 rhs=hT[:, c, half * 512:half * 512 + 512],
                                                 start=(c == 0), stop=(c == 15)),
                         R=[w_r] + hT_r[4 * half:4 * half + 4], W=[preg[BQ]])

            def q_evac():
                P.act(lambda e: e.activation(out=qsq[:], in_=bank_ap(BQ), func=AF.Square), R=[preg[BQ]], W=[qsq_r])
                P.act(lambda e: e.copy(out=qraw[:], in_=bank_ap(BQ)), R=[preg[BQ]], W=[qraw_r])

            def q_norm(half):
                P.pe(lambda e: e.matmul(bank_ap(BN), lhsT=onesbd[:], rhs=qsq[:], start=True, stop=True),
                     R=[qsq_r, const_r], W=[preg[BN]])
                P.act(lambda e: e.activation(out=qln[:], in_=bank_ap(BN), func=AF.Ln, bias=RMS_EPS, scale=1.0 / 64),
                      R=[preg[BN]], W=[qln_r])
                P.act(lambda e: e.activation(out=qln[:], in_=qln[:], func=AF.Exp, scale=-0.5), R=[qln_r], W=[qln_r])
                P.dve(lambda e: e.scalar_tensor_tensor(out=qn_t[:, half * 512:half * 512 + 512], in0=qraw[:],
                                                       scalar=qkg[:, 0:1], in1=qln[:], op0=ALU.mult, op1=ALU.mult),
                      R=[qraw_r, qln_r, const_r], W=[qn_r])

            def g_evac():
                P.act(lambda e: e.activation(out=bth[:], in_=bank_ap(BQ), func=AF.Exp, scale=-1.0),
                      R=[preg[BQ]], W=[bth_r])
                P.act(lambda e: e.activation(out=bth[:], in_=bth[:], func=AF.Ln, bias=1.0), R=[bth_r], W=[bth_r])
                P.act(lambda e: e.activation(out=bth[:], in_=bth[:], func=AF.Exp, scale=-1.0), R=[bth_r], W=[bth_r])
                P.dve(lambda e, half=half_box[0]: e.tensor_tensor(out=sg_t[:, half * 512:half * 512 + 512],
                                                                 in0=bank_ap(BQ), in1=bth[:], op=ALU.mult),
                      R=[preg[BQ], bth_r], W=[sg_r])

            def g_comb(half):
                pass

            def perm_atoms():
                at = []
                for r in range(4):
                    at.append(lambda r=r: P.act(lambda e: e.copy(
                        out=q4_t[:, r, :], in_=qn_t[:].rearrange("p (j r) -> p r j", r=4)[:, r, :]),
                        R=[qn_r], W=[q4_r]))
                for g in range(4):
                    at.append(lambda g=g: P.act(lambda e: e.copy(
                        out=q16_t[:, 4 * g:4 * g + 4, :],
                        in_=qn_t[:].rearrange("p (j r) -> p r j", r=16)[:, 4 * g:4 * g + 4, :]),
                        R=[qn_r], W=[q16_r]))
                at.append(lambda: P.act(lambda e: e.copy(
                    out=q16_t[:, 16:17, :], in_=qn_t[:, 0:64].rearrange("p (r j) -> p r j", r=1)),
                    R=[qn_r], W=[q16_r]))
                return at

            def mm2(kind, half, c0):
                wb, w_r = get_w(kind, hp)
                for c in range(c0, c0 + 2):
                    P.pe(lambda e, c=c: e.matmul(bank_ap(BQ), lhsT=wb[:, c, j * 128:(j + 1) * 128],
                                                 rhs=hT[:, c, half * 512:half * 512 + 512],
                                                 start=(c == 0), stop=(c == 15)),
                         R=[w_r] + hT_r[4 * half:4 * half + 4], W=[preg[BQ]])

            for half in range(2):
                hs_ = slice(half * 512, half * 512 + 512)
                for c0 in range(0, 16, 2):
                    mm2("q", half, c0)
                    yield
                P.act(lambda e: e.activation(out=qsq[:], in_=bank_ap(BQ), func=AF.Square), R=[preg[BQ]], W=[qsq_r])
                yield
                P.act(lambda e: e.copy(out=qraw[:], in_=bank_ap(BQ)), R=[preg[BQ]], W=[qraw_r])
                yield
                side = [
                    lambda: P.pe(lambda e: e.matmul(bank_ap(BN), lhsT=onesbd[:], rhs=qsq[:], start=True, stop=True),
                                 R=[qsq_r, const_r], W=[preg[BN]]),
                    lambda: P.act(lambda e: e.activation(out=qln[:], in_=bank_ap(BN), func=AF.Ln, bias=RMS_EPS,
                                                         scale=1.0 / 64), R=[preg[BN]], W=[qln_r]),
                    lambda: P.act(lambda e: e.activation(out=qln[:], in_=qln[:], func=AF.Exp, scale=-0.5),
                                  R=[qln_r], W=[qln_r]),
                    lambda hs_=hs_: P.dve(lambda e: e.scalar_tensor_tensor(
                        out=qn_t[:, hs_], in0=qraw[:], scalar=qkg[:, 0:1], in1=qln[:], op0=ALU.mult, op1=ALU.mult),
                        R=[qraw_r, qln_r, const_r], W=[qn_r]),
                ]
                pat = perm_atoms() if half == 1 else []
                side += pat[:4]
                for n_, c0 in enumerate(range(0, 16, 2)):
                    mm2("bg", half, c0)
                    yield
                    if n_ < len(side) and side[n_] is not None:
                        side[n_]()
                        yield
                P.act(lambda e: e.activation(out=bth[:], in_=bank_ap(BQ), func=AF.Exp, scale=-1.0),
                      R=[preg[BQ]], W=[bth_r])
                yield
                P.act(lambda e: e.activation(out=bth[:], in_=bth[:], func=AF.Ln, bias=1.0), R=[bth_r], W=[bth_r])
                yield
                P.act(lambda e: e.activation(out=bth[:], in_=bth[:], func=AF.Exp, scale=-1.0), R=[bth_r], W=[bth_r])
                yield
                P.dve(lambda e, hs_=hs_: e.tensor_tensor(out=sg_t[:, hs_], in0=bank_ap(BQ), in1=bth[:], op=ALU.mult),
                      R=[preg[BQ], bth_r], W=[sg_r])
                yield
                for atom in pat[4:]:
                    atom()
                    yield

        def run_all(gen):
            for _ in gen:
                pass

        def sbank():
            ctr["sb"] += 1
            return ctr["sb"] % 4

        def obank():
            ctr["ob"] += 1
            return 4 + ctr["ob"] % 2

        groups = []

        def add_call(hp, h, units, QN, mask, evac, half=False):
            per_s = 512 // (2 * QN)
            call = dict(ob=None)
            ng = len(units) // per_s
            for gi in range(ng):
                groups.append(dict(hp=hp, h=h, QN=QN, mask=mask, call=call, g0=gi * per_s, half=half,
                                   units=units[gi * per_s:(gi + 1) * per_s], evac=evac if gi == ng - 1 else None,
                                   fin=None))

        for hp in range(8):
            for h in range(2):
                a_t, a_r = acc[h]
                for half in range(2):
                    units = [(("n", qt * 128), TB * 2 - 128 + qt * 128, TB * 2 + qt * 128, 1, qt, qt + 1)
                             for qt in range(4 * half, 4 * half + 4)]
                    add_call(hp, h, units, 128, mask128,
                             lambda ob, half=half, a_t=a_t, a_r=a_r: P.act(
                                 lambda e: e.copy(out=a_t[:, half * 512:half * 512 + 512], in_=bank_ap(ob)[0:65, :]),
                                 R=[preg[ob]], W=[a_r]))
                for m in range(2):
                    units = [(("4", r * 256 + 128 * m), 2 * TB + r + 512 * (m - 1), 2 * TB + r + 512 * m, 4,
                              9 + 3 * r + m, 9 + 3 * r + m + 1) for r in range(4)]

                    def ev4(ob, m=m, a_t=a_t, a_r=a_r):
                        av = a_t[:, m * 512:m * 512 + 512].rearrange("p (j r) -> p r j", r=4)
                        P.dve(lambda e: e.tensor_tensor(
                            out=av, in0=bank_ap(ob)[0:65, :].rearrange("p (r j) -> p r j", r=4), in1=av, op=ALU.add),
                            R=[preg[ob], a_r], W=[a_r])
                    add_call(hp, h, units, 128, mask128, ev4)
                for g in range(4):
                    units = [(("16", r * 64), r, TB + r, 16, 21 + r, 37 + r) for r in range(4 * g, 4 * g + 4)]

                    def ev16(ob, g=g, a_t=a_t, a_r=a_r):
                        av = a_t[:, :].rearrange("p (j r) -> p r j", r=16)[:, 4 * g:4 * g + 4, :]
                        src = bank_ap(ob)[0:65, :].rearrange("p (r jj) -> p r jj", r=4)[:, :, 0:64]
                        P.dve(lambda e: e.tensor_tensor(out=av, in0=src, in1=av, op=ALU.add),
                              R=[preg[ob], a_r], W=[a_r])
                    add_call(hp, h, units, 128, mask16, ev16, half=True)
                groups[-1]["fin"] = (hp, h)
        for gi_, G_ in enumerate(groups):
            G_["gi"] = gi_

        def finish_head_dma(hp, h):
            a_t, a_r = acc[h]
            ctr["ds"] += 1
            di = ctr["ds"] % 4
            dno, dno_r = dnos[h]
            P.dma("sp", d_s[di], a_t[64:65, :], R=[a_r], W=[ds_r[di]])
            P.dma("sp", dno[0:64, :], d_s[di].partition_broadcast(64), R=[ds_r[di]], W=[dno_r])

        def finish_head_atoms(hp, h):
            a_t, a_r = acc[h]
            hb = 64 * h
            sg_t, sg_r = sgbb[hp % 3]
            dno, dno_r = dnos[h]
            atoms = []
            for hf in range(2):
                cs = slice(hf * 512, hf * 512 + 512)
                atoms.append(lambda cs=cs: P.act(lambda e: e.activation(out=dno[0:64, cs], in_=dno[0:64, cs], func=AF.Ln),
                                                 R=[dno_r], W=[dno_r]))
                atoms.append(lambda cs=cs: P.act(lambda e: e.activation(out=dno[0:64, cs], in_=dno[0:64, cs], func=AF.Exp,
                                                                        scale=-1.0), R=[dno_r], W=[dno_r]))
            for hf in range(2):
                cs = slice(hf * 512, hf * 512 + 512)
                atoms.append(lambda cs=cs: P.dve(lambda e: e.tensor_tensor(
                    out=dno[hb:hb + 64, cs], in0=a_t[0:64, cs], in1=dno[0:64, cs], op=ALU.mult),
                    R=[a_r, dno_r], W=[dno_r]))
                atoms.append(lambda cs=cs: P.dve(lambda e: e.tensor_tensor(
                    out=mixT[hb:hb + 64, 8 + hp, cs], in0=dno[hb:hb + 64, cs], in1=sg_t[hb:hb + 64, cs], op=ALU.mult),
                    R=[dno_r, sg_r], W=[mix_r[8 + hp]]))
            return atoms

        def emit_S(G):
            hp, h, QN = G["hp"], G["h"], G["QN"]
            (k_t, k_r) = kTw[hp % 2]
            hb = 64 * h
            sb = groups.index(G) % 4 if "gi" not in G else G["gi"] % 4
            G["sb"] = sb
            i = hp % 2
            for uu, ((qlay, q0), kA0, kB0, ks, vA, vB) in enumerate(G["units"]):
                base = uu * 2 * QN
                if qlay == "n":
                    rhs, q_r = qTn[i][0][hb:hb + 64, q0:q0 + QN], qTn[i][1]
                elif qlay == "4":
                    rhs, q_r = q4b[i][0][hb:hb + 64].rearrange("p r j -> p (r j)")[:, q0:q0 + QN], q4b[i][1]
                else:
                    rhs, q_r = q16b[i][0][hb:hb + 64].rearrange("p r j -> p (r j)")[:, q0:q0 + QN], q16b[i][1]
                P.pe(lambda e, base=base, rhs=rhs, kA0=kA0, ks=ks: e.matmul(
                    bank_ap(sb)[:, base:base + QN], lhsT=k_t[hb:hb + 64, sl(kA0, 128, ks)], rhs=rhs,
                    start=True, stop=True), R=[k_r, q_r], W=[preg[sb]])
                P.pe(lambda e, base=base, rhs=rhs, kB0=kB0, ks=ks: e.matmul(
                    bank_ap(sb)[:, base + QN:base + 2 * QN], lhsT=k_t[hb:hb + 64, sl(kB0, 128, ks)], rhs=rhs,
                    start=True, stop=True), R=[k_r, q_r], W=[preg[sb]])

        def emit_expmask2(GA, GB):
            sbA, sbB = GA["sb"], GB["sb"]
            assert sbB == sbA + 1 and sbA % 2 == 0 and GA["mask"] is GB["mask"] and GA["half"] == GB["half"]
            mask = GA["mask"]
            ctr["pt"] = ctr.get("pt", 0) + 1
            p_t, p_r = pt[ctr["pt"] % 3]
            GA["pt"] = (p_t[:, 0:512], p_r)
            GB["pt"] = (p_t[:, 512:1024], p_r)
            src = PP[sbA // 2][:, :]
            m2 = mask[:, :].unsqueeze(1).to_broadcast([128, 2, 512])
            if GA["half"]:
                v = lambda ap: ap.rearrange("p (u j) -> p u j", j=128)[:, :, 0:64]
                mv = m2.rearrange("p t (u j) -> p t u j", j=128)[:, :, :, 0:64]
                pv = lambda ap: ap.rearrange("p (t u j) -> p t u j", t=2, j=128)[:, :, :, 0:64]
            else:
                v = lambda ap: ap
                mv = m2
                pv = lambda ap: ap.rearrange("p (t c) -> p t c", t=2)
            P.act(lambda e: e.activation(out=v(p_t[:]), in_=v(src), func=AF.Exp, scale=0.125),
                  R=[preg[sbA], preg[sbB]], W=[p_r])
            P.dve(lambda e: e.tensor_tensor(out=pv(p_t[:]), in0=pv(p_t[:]), in1=mv, op=ALU.mult),
                  R=[p_r, const_r], W=[p_r])

        def emit_PV(G):
            hp, h, QN = G["hp"], G["h"], G["QN"]
            (v_t, _v) = vt[hp % 2]
            vr = vparts[hp % 2]
            call = G["call"]
            if call["ob"] is None:
                call["ob"] = obank()
            ob = call["ob"]
            p_t, p_r = G["pt"]
            for uu, (_q, kA0, kB0, ks, vA, vB) in enumerate(G["units"]):
                base = uu * 2 * QN
                oc = (G["g0"] + uu) * QN
                P.pe(lambda e, oc=oc, base=base, vA=vA: e.matmul(
                    bank_ap(ob)[0:65, oc:oc + QN], lhsT=v_t[:, vA, h, :], rhs=p_t[:, base:base + QN],
                    start=True, stop=False), R=vr + [p_r], W=[preg[ob]])
                P.pe(lambda e, oc=oc, base=base, vB=vB: e.matmul(
                    bank_ap(ob)[0:65, oc:oc + QN], lhsT=v_t[:, vB, h, :], rhs=p_t[:, base + QN:base + 2 * QN],
                    start=False, stop=True), R=vr + [p_r], W=[preg[ob]])

        deferred = []

        def emit_evac(G, it):
            if G["evac"] is not None:
                G["evac"](G["call"]["ob"])
            if G["fin"] is not None:
                fhp, fh = G["fin"]
                finish_head_dma(fhp, fh)
                for k_, atom in enumerate(finish_head_atoms(fhp, fh)):
                    deferred.append((it + 5 + k_ // 2, atom))
                if fh == 1 and fhp + 2 < 8:
                    load_kv(fhp + 2)

        load_kv(0)
        load_kv(1)
        run_all(prep_gen(0))
        n = len(groups)
        GP = 2
        nit = (n + GP - 1) // GP
        IT_HP = 32 // GP
        gen = None
        for it in range(nit + 3):
            if it % IT_HP == 0 and it < nit:
                if gen is not None:
                    run_all(gen)
                nxt = it // IT_HP + 1
                gen = prep_gen(nxt) if nxt < 8 else None
            odd = it % 2 == 1
            if not odd:
                for gi in range(it * GP, it * GP + GP):
                    if 0 <= gi < n:
                        emit_S(groups[gi])
            if 0 <= (it - 1) * GP < n:
                emit_expmask2(groups[(it - 1) * GP], groups[(it - 1) * GP + 1])
            for gi in range((it - 2) * GP, (it - 2) * GP + GP):
                if 0 <= gi < n:
                    emit_PV(groups[gi])
            for gi in range((it - 3) * GP, (it - 3) * GP + GP):
                if 0 <= gi < n:
                    emit_evac(groups[gi], it)
            if gen is not None:
                for _ in range(4):
                    next(gen, None)
            while deferred and deferred[0][0] <= it:
                deferred.pop(0)[1]()
            if odd:
                for gi in range(it * GP, it * GP + GP):
                    if 0 <= gi < n:
                        emit_S(groups[gi])
        for _, fn in deferred:
            fn()
        runB["regs"] = ([qsq_r, qraw_r, qln_r, bth_r, bhg_r] + [r_ for _, r_ in kTw] + [r_ for _, r_ in vt]
                        + [r_ for part in vparts for r_ in part] + [r_ for _, r_ in pt])

    def stage_out(blk, nxt=None):
        A.reset(runB["start"])
        ot = [A.arena(f"ot{i}", [128, 512], F32) for i in range(2)]
        xr = [A.arena(f"xr{i}", [128, 512], F32) for i in range(4)]
        b0 = s0_alloc() if nxt is not None else None
        assert A.mark() <= runB["end"], (A.mark(), runB["end"])
        P.fence_regs(runB["regs"])
        bg = s0_gen(nxt, hT, hT_r, b0) if nxt is not None else None
        PS["split"] = bg is not None
        n = 0

        def ld(cb, tt, n):
            x_t, x_r = xr[n % 4]
            row0 = blk * TB + tt * 128
            P.dma("sp", x_t[:], xe[row0:row0 + 128, cb * 512:cb * 512 + 512], W=[x_r])
        for n0 in range(3):
            ld(n0 // 8, n0 % 8, n0)
        for cb in range(4):
            wb, w_r = load_w(w_out[:, cb * 512:cb * 512 + 512])
            for tt in range(8):
                nn = n + 3
                if nn < 32:
                    ld(nn // 8, nn % 8, nn)
                b = next_bank()
                for c in range(16):
                    P.pe(lambda e, b=b, c=c, tt=tt, wb=wb: e.matmul(
                        bank_ap(b), lhsT=mixT[:, c, tt * 128:(tt + 1) * 128], rhs=wb[:, c, :],
                        start=(c == 0), stop=(c == 15)), R=[w_r, mix_r[c]], W=[preg[b]])
                x_t, x_r = xr[n % 4]
                o_t, o_r = ot[n % 2]
                P.dve(lambda e, b=b, x_t=x_t, o_t=o_t: e.tensor_tensor(out=o_t[:], in0=bank_ap(b), in1=x_t[:], op=ALU.add),
                      R=[preg[b], x_r], W=[o_r])
                row0 = (blk - 2) * TB + tt * 128
                out_ops.append(P.dma("sp", out_d[row0:row0 + 128, cb * 512:cb * 512 + 512], o_t[:], R=[o_r]))
                n += 1
                if bg is not None and (n == 1 or (n >= 16 and n % 2 == 0)):
                    next(bg, None)
        if bg is not None:
            run_all(bg)
        PS["split"] = False

    HA, HB = (hT, hT_r), (hTB, hTB_r)
    A.reset()
    b0 = s0_alloc()
    P.fence()
    g0 = s0_gen(0, *HA, b0, deep=True)
    next(g0)
    load_consts()
    run_all(g0)
    finish_consts()
    A.reset()
    kvb, chb, b0 = kv_alloc(), ch_alloc(), s0_alloc()
    P.fence()
    H.update(t=HA[0], r=HA[1])
    PS["split"] = True
    g = s0_gen(1, *HB, b0)
    stage_kv(0, kvb, bg=g)
    run_all(g)
    H.update(t=HB[0], r=HB[1])
    g = s0_gen(2, *HA, b0)
    stage_kv(1, kvb, bg=g)
    stage_conv_halo(chb, bgen=g)
    run_all(g)
    PS["split"] = False
    H.update(t=HA[0], r=HA[1])
    for blk in (2, 3):
        A.reset()
        kvb = kv_alloc()
        P.fence()
        stage_kv(blk, kvb)
        stage_conv(blk)
        stage_attn(blk)
        stage_out(blk, nxt=3 if blk == 2 else None)
    P.emit(final_wait_ops=out_ops)
    return nc


def _consts():
    bf = ml_dtypes.bfloat16
    a = np.arange(128)[:, None]
    j = np.arange(128)[None, :]
    mA = (a >= j).astype(np.float32)
    mB = (a <= j).astype(np.float32)
    m128 = np.concatenate([mA, mB, mA, mB], axis=1)
    jj = np.arange(128)[None, :]
    real = jj < 64
    mA16 = ((a >= jj) & real).astype(np.float32)
    mB16 = ((a >= 64) & ((a - 64) <= jj) & real).astype(np.float32)
    m16 = np.concatenate([mA16, mB16, mA16, mB16], axis=1)
    onesbd = np.zeros((128, 128), np.float32)
    onesbd[:64, :64] = 1
    onesbd[64:, 64:] = 1
    return dict(mask128=m128.astype(bf), mask16=m16.astype(bf), ident=np.eye(128, dtype=np.float32).astype(bf),
                onesbd=onesbd.astype(bf), onesf=np.ones((128, 128), np.float32).astype(bf))


def kernel(x, norm_g, w_in, conv_w, conv_b, conv_norm_g, conv_norm_b, conv_pw_w, conv_pw_b,
           q_norm_g, k_norm_g, w_out):
    f = lambda a: np.ascontiguousarray(np.asarray(a, dtype=np.float32))
    x = f(x)
    cm = lambda v: np.ascontiguousarray(f(v)[0].reshape(8, 128).T)
    shared = dict(
        gb=np.ascontiguousarray(np.broadcast_to(f(norm_g)[0], (128, D))),
        w_in=f(w_in)[0],
        convw=np.ascontiguousarray(f(conv_w)[0].reshape(31, 8, 128).transpose(2, 1, 0)),
        vec=np.ascontiguousarray(np.stack([cm(conv_b), cm(conv_norm_g), cm(conv_norm_b), cm(conv_pw_b)], axis=1)),
        qkg=np.ascontiguousarray(np.stack([np.tile(f(q_norm_g)[0], 2), np.tile(f(k_norm_g)[0], 2)], axis=1)),
        pw=f(conv_pw_w)[0],
        w_out=f(w_out)[0],
        **_consts(),
    )
    in_maps = []
    for c in range(8):
        b, half = c // 2, c % 2
        main = x[b, half * TPC:(half + 1) * TPC]
        halo = x[b, 0:TPC] if half == 1 else np.zeros((TPC, D), np.float32)
        m = dict(shared)
        m["xe"] = np.ascontiguousarray(np.concatenate([halo, main], axis=0))
        m["flag"] = np.full((128, 1), float(half), np.float32)
        in_maps.append(m)
    nc = build_program()
    res = run_bass_kernel_spmd(nc, in_maps, core_ids=list(range(8)))
    out = np.empty((NB, S, D), np.float32)
    for c in range(8):
        b, half = c // 2, c % 2
        out[b, half * TPC:(half + 1) * TPC] = res.results[c]["out"]
    return out
```

```python
import numpy as np
import ml_dtypes
from contextlib import ExitStack
import concourse.bass as bass
import concourse.mybir as mybir
from concourse.bass_utils import run_bass_kernel_spmd

F32 = mybir.dt.float32
BF16 = mybir.dt.bfloat16
ALU = mybir.AluOpType
AF = mybir.ActivationFunctionType
COMPUTE = ("pe", "act", "dve", "pool")

D = 2048
S = 4096
NB = 4
TPC = 2048
TB = 1024
IN_W = 7168
COL = dict(val=0, glu=1024, gate=2048, q=3072, k=4096, v=5120, bg=6144)
RMS_EPS = 1e-6
LN_EPS = 1e-5
SB_BASE = 16512
SB_END = 229344
NPE = 16


def sl(start, count, step=1):
    return slice(start, start + (count - 1) * step + 1, step)


class Reg:
    __slots__ = ("name", "w", "r", "arena")

    def __init__(self, name, arena=False):
        self.name = name
        self.w = None
        self.r = []
        self.arena = arena


class Op:
    __slots__ = ("eng", "fn", "deps", "is_dma", "signal", "ticket", "sem", "idx")

    def __init__(self, eng, fn, is_dma):
        self.eng = eng
        self.fn = fn
        self.is_dma = is_dma
        self.deps = []
        self.signal = False
        self.ticket = None
        self.sem = None


class Prog:
    def __init__(self, nc):
        self.nc = nc
        self.ops = []
        self.n_dma_sems = {"sp": 24, "pool": 6, "act": 2}
        self.last_arena = {}
        self.arena_dmas = []
        self.pending = {}

    def fence(self):
        f = list(self.last_arena.values()) + self.arena_dmas
        self.arena_dmas = []
        for e in ("pe", "act", "dve", "pool", "sp"):
            old = [d for d in self.pending.get(e, []) if d.is_dma]
            self.pending[e] = old + f

    def fence_regs(self, regs):
        f = {}
        for r in regs:
            if r.w is not None:
                f[r.w.idx] = r.w
            for rd in r.r:
                f[rd.idx] = rd
        f = list(f.values())
        for e in ("pe", "act", "dve", "pool", "sp"):
            old = [d for d in self.pending.get(e, []) if d.is_dma]
            self.pending[e] = old + f

    def add(self, eng, fn, R=(), W=(), is_dma=False):
        op = Op(eng, fn, is_dma)
        op.idx = len(self.ops)
        deps = {}
        arena = False
        for r in R:
            arena |= r.arena
            if r.w is not None:
                deps[r.w.idx] = r.w
        for w in W:
            arena |= w.arena
            if w.w is not None:
                deps[w.w.idx] = w.w
            for rd in w.r:
                deps[rd.idx] = rd
        if arena:
            for d in self.pending.get(eng, ()):
                deps[d.idx] = d
            self.pending[eng] = []
        best = {}
        for d in deps.values():
            if d.is_dma:
                best[("dma", d.idx)] = d
            else:
                if (not is_dma) and d.eng == eng and eng == "pe":
                    continue
                b = best.get(d.eng)
                if b is None or d.idx > b.idx:
                    best[d.eng] = d
        for d in best.values():
            d.signal = True
            op.deps.append(d)
        for w in W:
            w.w = op
            w.r = []
        for r in R:
            if r.w is not op:
                if is_dma:
                    r.r.append(op)
                else:
                    r.r = [x for x in r.r if x.is_dma or x.eng != eng]
                    r.r.append(op)
        if arena:
            if is_dma:
                self.arena_dmas.append(op)
            else:
                self.last_arena[eng] = op
        self.ops.append(op)
        return op

    def pe(self, fn, R=(), W=()):
        return self.add("pe", fn, R, W)

    def act(self, fn, R=(), W=()):
        return self.add("act", fn, R, W)

    def dve(self, fn, R=(), W=()):
        return self.add("dve", fn, R, W)

    def pool(self, fn, R=(), W=()):
        return self.add("pool", fn, R, W)

    def dma(self, queue, out, in_, R=(), W=(), **kw):
        return self.add(queue, lambda e: e.dma_start(out=out, in_=in_, **kw), R, W, is_dma=True)

    def emit(self, final_wait_ops=()):
        nc = self.nc
        with ExitStack() as es:
            esem = {e: es.enter_context(nc.semaphore("s_" + e)) for e in COMPUTE}
            dsem = {q: [es.enter_context(nc.semaphore(f"d_{q}{i}")) for i in range(n)]
                    for q, n in self.n_dma_sems.items()}
            ecount = {e: 0 for e in COMPUTE}
            dcount = {q: [0] * n for q, n in self.n_dma_sems.items()}
            drr = {q: 0 for q in self.n_dma_sems}
            dprev = {}
            for op in self.ops:
                if op.is_dma:
                    q = op.eng
                    j = drr[q] % self.n_dma_sems[q]
                    drr[q] += 1
                    op.sem = dsem[q][j]
                    dprev[op.idx] = dcount[q][j]
                    dcount[q][j] += 16
                    op.ticket = dcount[q][j]
                elif op.signal:
                    ecount[op.eng] += 1
                    op.ticket = ecount[op.eng]
                    op.sem = esem[op.eng]
            by_eng = {}
            for op in self.ops:
                by_eng.setdefault(op.eng, []).append(op)
            final_wait_ops = list(final_wait_ops)

            def run(engname, e):
                waited = {}

                def wait(sem, val):
                    k = id(sem)
                    if waited.get(k, 0) >= val:
                        return
                    waited[k] = val
                    e.wait_ge(sem, val)

                for op in by_eng.get(engname, []):
                    for d in op.deps:
                        wait(d.sem, d.ticket)
                    if op.is_dma:
                        if dprev[op.idx] > 0:
                            wait(op.sem, dprev[op.idx])
                        op.fn(e).then_inc(op.sem, 16)
                    else:
                        ins = op.fn(e)
                        if op.signal:
                            ins.then_inc(op.sem, 1)
                if engname == "sp":
                    for op in final_wait_ops:
                        wait(op.sem, op.ticket)

            with nc.Block() as block:
                @block.sync
                def _(e):
                    run("sp", e)

                @block.tensor
                def _(e):
                    run("pe", e)

                @block.scalar
                def _(e):
                    run("act", e)

                @block.vector
                def _(e):
                    run("dve", e)

                @block.gpsimd
                def _(e):
                    run("pool", e)


class Alloc:
    def __init__(self, nc):
        self.nc = nc
        self.pp = SB_BASE
        self.abase = None
        self.ap_ = None
        self.ctr = 0
        self.peak = 0

    def _mk(self, name, shape, dt, off):
        self.ctr += 1
        return self.nc.alloc_sbuf_tensor_at(f"{name}_{self.ctr}", list(shape), dt, offset=off)

    @staticmethod
    def _bytes(shape, dt):
        n = 1
        for s in shape[1:]:
            n *= s
        n *= 2 if dt == BF16 else 4
        return (n + 31) // 32 * 32

    def persist(self, name, shape, dt):
        assert self.abase is None
        t = self._mk(name, shape, dt, self.pp)
        self.pp += self._bytes(shape, dt)
        return t

    def start_arena(self):
        self.abase = self.pp
        self.ap_ = self.pp

    def reset(self, to=None):
        self.ap_ = self.abase if to is None else to

    def mark(self):
        return self.ap_

    def arena(self, name, shape, dt, reg=True):
        t = self._mk(name, shape, dt, self.ap_)
        self.ap_ += self._bytes(shape, dt)
        self.peak = max(self.peak, self.ap_)
        assert self.ap_ <= SB_END, (name, self.ap_, SB_END)
        return (t, Reg(name, arena=True)) if reg else t


def build_program():
    nc = bass.Bass("TRN2", target_bir_lowering=False)
    dt_in = lambda name, shape, dt=F32: nc.dram_tensor(name, list(shape), dt, kind="ExternalInput").ap()
    xe = dt_in("xe", [2 * TPC, D])
    gb = dt_in("gb", [128, D])
    w_in = dt_in("w_in", [D, IN_W])
    convw_d = dt_in("convw", [128, 8, 31])
    vec_d = dt_in("vec", [128, 4, 8])
    qkg_d = dt_in("qkg", [128, 2])
    pw_d = dt_in("pw", [1024, 1024])
    w_out = dt_in("w_out", [D, D])
    flag_d = dt_in("flag", [128, 1])
    mask128_d = dt_in("mask128", [128, 512], BF16)
    mask16_d = dt_in("mask16", [128, 512], BF16)
    ident_d = dt_in("ident", [128, 128], BF16)
    onesbd_d = dt_in("onesbd", [128, 128], BF16)
    onesf_d = dt_in("onesf", [128, 128], BF16)
    out_d = nc.dram_tensor("out", [TPC, D], F32, kind="ExternalOutput").ap()
    kT_s = nc.dram_tensor("kT_s", [8, 128, 2 * TPC], BF16, kind="Internal").ap()
    v_s = nc.dram_tensor("v_s", [2 * TPC, 16, 65], BF16, kind="Internal").ap()
    d_s = nc.dram_tensor("d_s", [4, 1, TB], F32, kind="Internal").ap()

    P = Prog(nc)
    A = Alloc(nc)

    hT = A.persist("hT", [128, 16, TB], BF16)
    hT_r = [Reg(f"hT{t}") for t in range(8)]
    hT_off = SB_BASE
    wbuf, wbuf_pw, wreg = [], [], []
    for i in range(3):
        off = A.pp
        wbuf.append(A.persist(f"wbuf{i}", [128, 16, 512], BF16))
        wbuf_pw.append(A._mk(f"wbufpw{i}", [128, 8, 1024], BF16, off))
        wreg.append(Reg(f"wbuf{i}"))
    mix_off = A.pp
    mixT = A.persist("mixT", [128, 16, TB], BF16)
    mix_r = [Reg(f"mix{c}") for c in range(16)]
    hTB = A._mk("hTB", [128, 16, TB], BF16, mix_off)
    hTB_r = [Reg(f"hTB{t}", arena=True) for t in range(8)]
    H = dict(t=hT, r=hT_r)
    ident = A.persist("ident", [128, 128], BF16)
    onesbd = A.persist("onesbd", [128, 128], BF16)
    onesf = A.persist("onesf", [128, 128], BF16)
    mask128 = A.persist("mask128", [128, 512], BF16)
    mask16 = A.persist("mask16", [128, 512], BF16)
    convw = A.persist("convw", [128, 8, 31], F32)
    vec = A.persist("vec", [128, 4, 8], F32)
    vec05 = A.persist("vec05", [128, 2, 8], F32)
    qkg = A.persist("qkg", [128, 2], F32)
    flag = A.persist("flag", [128, 1], F32)
    u_tail = A.persist("u_tail", [128, 8, 30], F32)
    utail_r = [Reg(f"utail{c}") for c in range(8)]
    const_r = Reg("consts")
    A.start_arena()

    PP = [nc.alloc_psum_tensor(f"PP{i}", [128, 1024], F32) for i in range(4)]
    PPb = [t.bitcast(BF16) for t in PP]
    preg = [Reg(f"bank{i}") for i in range(8)]

    def bank_ap(b):
        return PP[b // 2][:, (b % 2) * 512:(b % 2) * 512 + 512]

    def bank_bf(b):
        return PPb[b // 2][:, (b % 2) * 1024:(b % 2) * 1024 + 1024].rearrange("p (c t) -> p c t", c=8)

    ctr = dict(pair=0, bank=0, w=0, sb=0, ob=0, ds=0)

    PS = dict(split=False)

    def next_pair():
        ctr["pair"] += 1
        return ctr["pair"] % (3 if PS["split"] else 4)

    def next_bank():
        ctr["bank"] += 1
        return ctr["bank"] % (6 if PS["split"] else 8)

    def s0_bank():
        ctr["s0b"] = ctr.get("s0b", 0) + 1
        return 6 + ctr["s0b"] % 2

    def load_w(src, pw=False):
        i = ctr["w"] % 3
        ctr["w"] += 1
        dst = wbuf_pw[i] if pw else wbuf[i]
        P.dma("pool", dst[:], src.rearrange("(c p) n -> p c n", p=128), W=[wreg[i]])
        return dst, wreg[i]

    cregs = []
    ident_r = Reg("ident")
    P.dma("sp", ident[:], ident_d, W=[ident_r])
    rest_consts = ((onesbd, onesbd_d), (onesf, onesf_d), (mask128, mask128_d),
                   (mask16, mask16_d), (convw, convw_d), (vec, vec_d), (qkg, qkg_d), (flag, flag_d))

    def load_consts():
        for dst, src in rest_consts:
            cregs.append(Reg("c%d" % len(cregs)))
            P.dma("sp", dst[:], src, W=[cregs[-1]])

    def finish_consts():
        P.dve(lambda e: e.tensor_scalar(out=vec05[:], in0=vec[:, 1:3, :], scalar1=0.5, scalar2=None, op0=ALU.mult),
              R=cregs + [ident_r], W=[const_r])

    kTs_r = [[Reg(f"kTs{hp}_{b}") for b in range(NB)] for hp in range(8)]
    vs_r = [Reg(f"vs{b}") for b in range(NB)]
    ds_r = [Reg(f"ds{i}") for i in range(4)]
    out_ops = []

    def s0_alloc():
        return dict(
            xt=[A.arena(f"xt{i}", [128, D], F32) for i in range(3)],
            xn=[A.arena(f"xn{i}", [128, D], BF16) for i in range(2)],
            gbt=A.arena("gbt", [128, D], F32),
            ss=[A.arena(f"ss{i}", [128, 1], F32) for i in range(2)],
            rs=[A.arena(f"rs{i}", [128, 1], F32) for i in range(2)])

    def s0_gen(blk, hT, hT_r, bufs, deep=False):
        xt, xn, ss, rs = bufs["xt"], bufs["xn"], bufs["ss"], bufs["rs"]
        gbt, gbt_r = bufs["gbt"]
        P.dma("sp", gbt[:], gb, W=[gbt_r])
        banks = {}

        def stL(tt):
            (x_t, x_r) = xt[tt % 3]
            row0 = blk * TB + tt * 128
            P.dma("sp", x_t[:], xe[row0:row0 + 128, :], W=[x_r])

        def stA(tt):
            i = tt % 2
            (x_t, x_r), (s_t, s_r), (r_t, r_r) = xt[tt % 3], ss[i], rs[i]
            (j_t, j_r) = xn[i]
            P.act(lambda e: e.activation(out=j_t[:], in_=x_t[:], func=AF.Square, accum_out=s_t[:]),
                  R=[x_r], W=[j_r, s_r])
            P.act(lambda e: e.activation(out=r_t[:], in_=s_t[:], func=AF.Ln, bias=RMS_EPS, scale=1.0 / D),
                  R=[s_r], W=[r_r])
            P.act(lambda e: e.activation(out=r_t[:], in_=r_t[:], func=AF.Exp, scale=-0.5), R=[r_r], W=[r_r])

        def stB(tt):
            i = tt % 2
            (x_t, x_r), (n_t, n_r), (r_t, r_r) = xt[tt % 3], xn[i], rs[i]
            P.dve(lambda e: e.scalar_tensor_tensor(
                out=n_t[:], in0=x_t[:], scalar=r_t[:], in1=gbt[:], op0=ALU.mult, op1=ALU.mult),
                R=[x_r, r_r, gbt_r], W=[n_r])
            banks[tt] = []
            for half in range(2):
                b = s0_bank()
                banks[tt].append(b)
                for c8 in range(8):
                    c = half * 8 + c8
                    P.pe(lambda e, b=b, c8=c8, c=c: e.transpose(
                        out=bank_bf(b)[:, c8, :], in_=n_t[:, c * 128:(c + 1) * 128], identity=ident[:]),
                        R=[n_r, ident_r], W=[preg[b]])

        def stC(tt):
            for half in range(2):
                b = banks[tt][half]
                dst = hT[:, half * 8:half * 8 + 8, tt * 128:(tt + 1) * 128]
                if half == 0:
                    P.act(lambda e, b=b, dst=dst: e.copy(out=dst, in_=bank_bf(b)), R=[preg[b]], W=[hT_r[tt]])
                else:
                    P.dve(lambda e, b=b, dst=dst: e.tensor_copy(out=dst, in_=bank_bf(b)), R=[preg[b]], W=[hT_r[tt]])

        stL(0)
        if deep:
            stL(1)
        for it in range(8 + 1):
            if not deep and it + 1 < 8:
                stL(it + 1)
            if it < 8:
                stA(it)
            if 0 <= it - 1 < 8:
                stB(it - 1)
                stC(it - 1)
            if deep and it + 2 < 8:
                stL(it + 2)
            yield

    def run_all(gen):
        for _ in gen:
            pass

    def proj_fm(wb, w_r, j, p=None):
        if p is None:
            p = next_pair()
        hT, hT_r = H["t"], H["r"]
        for half in range(2):
            for c in range(16):
                P.pe(lambda e, p=p, half=half, c=c: e.matmul(
                    PP[p][:, half * 512:half * 512 + 512], lhsT=wb[:, c, j * 128:(j + 1) * 128],
                    rhs=hT[:, c, half * 512:half * 512 + 512], start=(c == 0), stop=(c == 15)),
                    R=[w_r] + hT_r[4 * half:4 * half + 4], W=[preg[2 * p + half]])
        return p

    def pr(p):
        return [preg[2 * p], preg[2 * p + 1]]

    def head_norm(p, gcol, tmp, dst, dst_regs, p2=None):
        (sq_t, sq_r), (ln_t, ln_r) = tmp
        P.act(lambda e: e.activation(out=sq_t[:], in_=PP[p][:], func=AF.Square), R=pr(p), W=[sq_r])
        if p2 is None:
            p2 = next_pair()
        for half in range(2):
            P.pe(lambda e, half=half: e.matmul(PP[p2][:, half * 512:half * 512 + 512], lhsT=onesbd[:],
                                               rhs=sq_t[:, half * 512:half * 512 + 512], start=True, stop=True),
                 R=[sq_r, const_r], W=[preg[2 * p2 + half]])
        P.act(lambda e: e.activation(out=ln_t[:], in_=PP[p2][:], func=AF.Ln, bias=RMS_EPS, scale=1.0 / 64),
              R=pr(p2), W=[ln_r])
        P.act(lambda e: e.activation(out=ln_t[:], in_=ln_t[:], func=AF.Exp, scale=-0.5), R=[ln_r], W=[ln_r])
        P.dve(lambda e: e.scalar_tensor_tensor(out=dst, in0=PP[p][:], scalar=qkg[:, gcol:gcol + 1], in1=ln_t[:],
                                               op0=ALU.mult, op1=ALU.mult),
              R=pr(p) + [ln_r, const_r], W=dst_regs)

    def kv_alloc():
        return dict(vnat=A.arena("vnat", [128, 8, 16, 65], BF16),
                    vregs=[Reg(f"vnat{t}", arena=True) for t in range(8)],
                    tmps=[(A.arena(f"ksq{i}", [128, TB], BF16), A.arena(f"kln{i}", [128, TB], F32)) for i in range(2)],
                    kTo=[A.arena(f"kTo{i}", [128, TB], BF16) for i in range(2)])

    def stage_kv(blk, bufs, bg=None):
        vnat, vnat_r = bufs["vnat"]
        vregs = bufs["vregs"]
        tmps, kTo = bufs["tmps"], bufs["kTo"]
        hT, hT_r = H["t"], H["r"]
        P.dve(lambda e: e.memset(vnat[:, :, :, 64:65], 1.0), W=vregs)
        if blk < 2:
            P.dve(lambda e: e.tensor_scalar(out=vnat[:, :, :, 64:65], in0=vnat[:, :, :, 64:65], scalar1=flag[:, 0:1],
                                            scalar2=None, op0=ALU.mult), R=[const_r], W=vregs)
        for cb in range(2):
            wb, w_r = load_w(w_in[:, COL["v"] + cb * 512:COL["v"] + cb * 512 + 512])
            for tt in range(8):
                b = next_bank()
                for c in range(16):
                    P.pe(lambda e, b=b, c=c, tt=tt, wb=wb: e.matmul(
                        bank_ap(b), lhsT=hT[:, c, tt * 128:(tt + 1) * 128], rhs=wb[:, c, :],
                        start=(c == 0), stop=(c == 15)), R=[w_r, hT_r[tt]], W=[preg[b]])
                dst = vnat[:, tt, cb * 8:cb * 8 + 8, 0:64]
                src = lambda b=b: bank_ap(b).rearrange("p (h e) -> p h e", e=64)
                if tt % 2 == 0:
                    P.act(lambda e, dst=dst, src=src: e.copy(out=dst, in_=src()), R=[preg[b]], W=[vregs[tt]])
                else:
                    P.dve(lambda e, dst=dst, src=src: e.tensor_copy(out=dst, in_=src()), R=[preg[b]], W=[vregs[tt]])
                if bg is not None and tt % 4 == 3:
                    next(bg, None)
        P.dma("sp", v_s[blk * TB:(blk + 1) * TB].rearrange("(t p) h e -> p t (h e)", p=128),
              vnat[:].rearrange("p t h e -> p t (h e)"), R=vregs, W=[vs_r[blk]])
        wk = [load_w(w_in[:, COL["k"] + wi * 512:COL["k"] + wi * 512 + 512]) for wi in range(2)]
        p_next = proj_fm(wk[0][0], wk[0][1], 0)
        for hp in range(8):
            p = p_next
            pipelined = not PS["split"]
            if hp + 1 < 8 and pipelined:
                wb, w_r = wk[(hp + 1) // 4]
                p_next = proj_fm(wb, w_r, (hp + 1) % 4)
            ko, ko_r = kTo[hp % 2]
            head_norm(p, 1, tmps[hp % 2], ko[:], [ko_r])
            if hp + 1 < 8 and not pipelined:
                wb, w_r = wk[(hp + 1) // 4]
                p_next = proj_fm(wb, w_r, (hp + 1) % 4)
            P.dma("sp", kT_s[hp, :, blk * TB:(blk + 1) * TB], ko[:], R=[ko_r], W=[kTs_r[hp][blk]])
            if bg is not None:
                next(bg, None)

    def act_gate(pg, th, hg=None):
        (th_t, th_r) = th
        P.act(lambda e: e.activation(out=th_t[:], in_=PP[pg][:], func=AF.Tanh, scale=0.5), R=pr(pg), W=[th_r])
        if hg is not None:
            (hg_t, hg_r) = hg
            P.act(lambda e: e.activation(out=hg_t[:], in_=PP[pg][:], func=AF.Copy, scale=0.5), R=pr(pg), W=[hg_r])

    def ch_alloc():
        return dict(th=A.arena("th", [128, 128], F32), hv=A.arena("hv", [128, 128], F32))

    def stage_conv_halo(bufs, bgen=None):
        th, th_r = bufs["th"]
        hv, hv_r = bufs["hv"]
        hT, hT_r = H["t"], H["r"]
        for wi in range(2):
            wv, wv_r = load_w(w_in[:, COL["val"] + wi * 512:COL["val"] + wi * 512 + 512])
            wg, wg_r = load_w(w_in[:, COL["glu"] + wi * 512:COL["glu"] + wi * 512 + 512])
            for j in range(4):
                c = wi * 4 + j
                bv, bg = next_bank(), next_bank()
                for (b, wb, w_r) in ((bv, wv, wv_r), (bg, wg, wg_r)):
                    for ck in range(16):
                        P.pe(lambda e, b=b, wb=wb, ck=ck, j=j: e.matmul(
                            bank_ap(b)[:, 0:128], lhsT=wb[:, ck, j * 128:(j + 1) * 128], rhs=hT[:, ck, 896:1024],
                            start=(ck == 0), stop=(ck == 15)), R=[w_r, hT_r[7]], W=[preg[b]])
                P.act(lambda e, bg=bg: e.activation(out=th[:], in_=bank_ap(bg)[:, 0:128], func=AF.Tanh, scale=0.5),
                      R=[preg[bg]], W=[th_r])
                P.act(lambda e, bv=bv: e.activation(out=hv[:], in_=bank_ap(bv)[:, 0:128], func=AF.Copy, scale=0.5),
                      R=[preg[bv]], W=[hv_r])
                P.dve(lambda e, c=c: e.scalar_tensor_tensor(out=u_tail[:, c, :], in0=th[:, 98:128], scalar=1.0,
                                                            in1=hv[:, 98:128], op0=ALU.add, op1=ALU.mult),
                      R=[th_r, hv_r], W=[utail_r[c]])
                if bgen is not None and c % 2 == 1:
                    next(bgen, None)

    def stage_conv(blk):
        A.reset()
        y = A.arena("y", [128, 8, TB], F32, reg=False)
        y_r = [Reg(f"y{c}", arena=True) for c in range(8)]
        mk = A.mark()
        u = [A.arena(f"u{i}", [128, TB + 30], F32) for i in range(2)]
        ubf = [A.arena(f"ubf{i}", [128, TB + 32], BF16) for i in range(2)]
        th = [A.arena(f"th{i}", [128, TB], F32) for i in range(2)]
        hv = [A.arena(f"hv{i}", [128, TB], F32) for i in range(2)]
        accA = [A.arena(f"accA{i}", [128, TB], F32) for i in range(2)]
        wd = [A.arena(f"wd{i}", [128, max(NPE, 1), 128], BF16) for i in range(2)]
        P.fence()
        tD = list(range(0, 31 - NPE))
        tP = list(range(31 - NPE, 31))
        wts = {}

        def load_pair(wi):
            wts[wi] = (load_w(w_in[:, COL["val"] + wi * 512:COL["val"] + wi * 512 + 512]),
                       load_w(w_in[:, COL["glu"] + wi * 512:COL["glu"] + wi * 512 + 512]))

        def proj_fixed(wb, w_r, j, p):
            for half in range(2):
                for ck in range(16):
                    P.pe(lambda e, half=half, ck=ck: e.matmul(
                        PP[p][:, half * 512:half * 512 + 512], lhsT=wb[:, ck, j * 128:(j + 1) * 128],
                        rhs=hT[:, ck, half * 512:half * 512 + 512], start=(ck == 0), stop=(ck == 15)),
                        R=[w_r] + hT_r[4 * half:4 * half + 4], W=[preg[2 * p + half]])

        def proj(c):
            if c // 4 not in wts:
                load_pair(c // 4)
            (wv, wv_r), (wg, wg_r) = wts[c // 4]
            proj_fixed(wv, wv_r, c % 4, 0)
            proj_fixed(wg, wg_r, c % 4, 1)

        def merge_pe(c):
            pc = 2 + c % 2
            yc = y[:, c, :]
            P.dve(lambda e: e.tensor_tensor(out=yc, in0=PP[pc][:], in1=yc, op=ALU.add),
                  R=pr(pc) + [y_r[c]], W=[y_r[c]])

        def build_wd(c):
            wd_t, wd_r = wd[c % 2]
            for n_, k in enumerate(tP):
                P.act(lambda e, n_=n_, k=k: e.activation(
                    out=wd_t[:, n_, :], in_=ident[:], func=AF.Copy, scale=convw[:, c, k:k + 1]),
                    R=[const_r], W=[wd_r])

        proj(0)
        if NPE:
            build_wd(0)
        for c in range(8):
            i = c % 2
            pv, pg = 0, 1
            (u_t, u_r), (ub_t, ub_r), (th_t, th_r), (hv_t, hv_r), (aa_t, aa_r), (wd_t, wd_r) = \
                u[i], ubf[i], th[i], hv[i], accA[i], wd[i]
            act_gate(pg, th[i])
            P.act(lambda e, hv_t=hv_t: e.activation(out=hv_t[:], in_=PP[pv][:], func=AF.Copy, scale=0.5),
                  R=pr(pv), W=[hv_r])
            if c + 1 < 8:
                proj(c + 1)
            P.dve(lambda e, u_t=u_t, c=c: e.tensor_copy(out=u_t[:, 0:30], in_=u_tail[:, c, :]),
                  R=[utail_r[c]], W=[u_r])
            P.dve(lambda e, u_t=u_t, th_t=th_t, hv_t=hv_t: e.scalar_tensor_tensor(
                out=u_t[:, 30:30 + TB], in0=th_t[:], scalar=1.0, in1=hv_t[:], op0=ALU.add, op1=ALU.mult),
                R=[th_r, hv_r], W=[u_r])
            P.dve(lambda e, u_t=u_t, c=c: e.tensor_copy(out=u_tail[:, c, :], in_=u_t[:, TB:TB + 30]),
                  R=[u_r], W=[utail_r[c]])
            yc = y[:, c, :]
            if NPE:
                P.act(lambda e, u_t=u_t, ub_t=ub_t: e.copy(out=ub_t[:, 0:TB + 30], in_=u_t[:]), R=[u_r], W=[ub_r])
                if c + 1 < 8:
                    build_wd(c + 1)
                pc = 2 + c % 2
                for half in range(2):
                    for n_, k in enumerate(tP):
                        P.pe(lambda e, half=half, n_=n_, k=k, pc=pc, wd_t=wd_t, ub_t=ub_t: e.matmul(
                            PP[pc][:, half * 512:half * 512 + 512], lhsT=wd_t[:, n_, :],
                            rhs=ub_t[:, k + half * 512:k + half * 512 + 512], start=(n_ == 0), stop=(n_ == NPE - 1)),
                            R=[wd_r, ub_r], W=[preg[2 * pc + half]])
            tA = tD[0::2]
            tA2 = tD[1::2]
            P.dve(lambda e, u_t=u_t, c=c, yc=yc: e.tensor_scalar(
                out=yc, in0=u_t[:, 0:TB], scalar1=convw[:, c, 0:1], scalar2=vec[:, 0, c:c + 1],
                op0=ALU.mult, op1=ALU.add), R=[u_r, const_r], W=[y_r[c]])
            P.dve(lambda e, u_t=u_t, c=c, aa_t=aa_t: e.tensor_scalar(
                out=aa_t[:], in0=u_t[:, 1:1 + TB], scalar1=convw[:, c, 1:2], scalar2=None, op0=ALU.mult),
                R=[u_r, const_r], W=[aa_r])
            for k in tD[2:]:
                if k in tA:
                    P.dve(lambda e, u_t=u_t, c=c, k=k, yc=yc: e.scalar_tensor_tensor(
                        out=yc, in0=u_t[:, k:k + TB], scalar=convw[:, c, k:k + 1], in1=yc,
                        op0=ALU.mult, op1=ALU.add), R=[u_r, const_r, y_r[c]], W=[y_r[c]])
                else:
                    P.dve(lambda e, u_t=u_t, c=c, k=k, aa_t=aa_t: e.scalar_tensor_tensor(
                        out=aa_t[:], in0=u_t[:, k:k + TB], scalar=convw[:, c, k:k + 1], in1=aa_t[:],
                        op0=ALU.mult, op1=ALU.add), R=[u_r, const_r, aa_r], W=[aa_r])
            P.dve(lambda e, yc=yc, aa_t=aa_t: e.tensor_tensor(out=yc, in0=yc, in1=aa_t[:], op=ALU.add),
                  R=[y_r[c], aa_r], W=[y_r[c]])
            if NPE and c >= 1:
                merge_pe(c - 1)
        if NPE:
            merge_pe(7)
        A.reset(mk)
        ybf = [A.arena(f"ybf{i}", [128, TB], BF16) for i in range(2)]
        ysq = [A.arena(f"ysq{i}", [128, TB], BF16) for i in range(2)]
        mean, mean_r = A.arena("mean", [128, TB], F32)
        rstd, rstd_r = A.arena("rstd", [128, TB], F32)
        th2 = [A.arena(f"th2{i}", [128, TB], F32) for i in range(2)]
        vp = [A.arena(f"vp{i}", [128, TB], F32) for i in range(2)]
        tmp, tmp_r = vp[0]
        sga = A.arena("sga", [128, 8, TB], BF16, reg=False)
        sga_r = [Reg(f"sga{c}", arena=True) for c in range(8)]
        gth = A.arena("gth", [128, TB], F32)
        ghg = A.arena("ghg", [128, TB], F32)
        P.fence()
        wg0 = load_w(w_in[:, COL["gate"]:COL["gate"] + 512])
        wg1 = load_w(w_in[:, COL["gate"] + 512:COL["gate"] + 1024])
        wpw, wpw_r = load_w(pw_d, pw=True)
        pg_pre = {c_: proj_fm(wg0[0], wg0[1], c_, p=c_) for c_ in range(2)}
        pS, pQ = 2, 3
        for c in range(8):
            i = c % 2
            (yb_t, yb_r), (ys_t, ys_r) = ybf[i], ysq[i]
            P.act(lambda e, c=c, yb_t=yb_t: e.copy(out=yb_t[:], in_=y[:, c, :]), R=[y_r[c]], W=[yb_r])
            P.act(lambda e, c=c, ys_t=ys_t: e.activation(out=ys_t[:], in_=y[:, c, :], func=AF.Square), R=[y_r[c]], W=[ys_r])
            for (p, t, r_) in ((pS, yb_t, yb_r), (pQ, ys_t, ys_r)):
                for half in range(2):
                    P.pe(lambda e, p=p, t=t, half=half, c=c: e.matmul(
                        PP[p][:, half * 512:half * 512 + 512], lhsT=onesf[:], rhs=t[:, half * 512:half * 512 + 512],
                        start=(c == 0), stop=(c == 7)), R=[r_, const_r], W=[preg[2 * p + half]])
        P.dve(lambda e: e.tensor_scalar(out=mean[:], in0=PP[pS][:], scalar1=1.0 / 1024, scalar2=None, op0=ALU.mult),
              R=pr(pS), W=[mean_r])
        P.dve(lambda e: e.tensor_tensor(out=tmp[:], in0=mean[:], in1=mean[:], op=ALU.mult), R=[mean_r], W=[tmp_r])
        P.dve(lambda e: e.scalar_tensor_tensor(out=tmp[:], in0=PP[pQ][:], scalar=1.0 / 1024, in1=tmp[:],
                                               op0=ALU.mult, op1=ALU.subtract), R=pr(pQ) + [tmp_r], W=[tmp_r])
        P.act(lambda e: e.activation(out=rstd[:], in_=tmp[:], func=AF.Ln, bias=LN_EPS), R=[tmp_r], W=[rstd_r])
        P.act(lambda e: e.activation(out=rstd[:], in_=rstd[:], func=AF.Exp, scale=-0.5), R=[rstd_r], W=[rstd_r])
        for c in range(8):
            i = c % 2
            (t2_t, t2_r), (vp_t, vp_r) = th2[i], vp[i]
            yc = y[:, c, :]
            wg, wg_r = (wg0, wg1)[c // 4]
            pg = pg_pre.pop(c) if c in pg_pre else proj_fm(wg, wg_r, c % 4)
            P.dve(lambda e, yc=yc: e.tensor_tensor(out=yc, in0=yc, in1=mean[:], op=ALU.subtract),
                  R=[y_r[c], mean_r], W=[y_r[c]])
            P.dve(lambda e, yc=yc: e.tensor_tensor(out=yc, in0=yc, in1=rstd[:], op=ALU.mult),
                  R=[y_r[c], rstd_r], W=[y_r[c]])
            P.act(lambda e, yc=yc, c=c, t2_t=t2_t: e.activation(out=t2_t[:], in_=yc, func=AF.Tanh,
                                                               scale=vec05[:, 0, c:c + 1], bias=vec05[:, 1, c:c + 1]),
                  R=[y_r[c], const_r], W=[t2_r])
            P.dve(lambda e, yc=yc, c=c, vp_t=vp_t: e.tensor_scalar(
                out=vp_t[:], in0=yc, scalar1=vec05[:, 0, c:c + 1], scalar2=vec05[:, 1, c:c + 1],
                op0=ALU.mult, op1=ALU.add), R=[y_r[c], const_r], W=[vp_r])
            P.dve(lambda e, c=c, t2_t=t2_t, vp_t=vp_t: e.scalar_tensor_tensor(
                out=mixT[:, 8 + c, :], in0=t2_t[:], scalar=1.0, in1=vp_t[:], op0=ALU.add, op1=ALU.mult),
                R=[t2_r, vp_r], W=[mix_r[8 + c]])
            act_gate(pg, gth, ghg)
            P.dve(lambda e, c=c: e.scalar_tensor_tensor(
                out=sga[:, c, :], in0=gth[0][:], scalar=1.0, in1=ghg[0][:], op0=ALU.add, op1=ALU.mult),
                R=[gth[1], ghg[1]], W=[sga_r[c]])
        for co in range(8):
            pp_ = next_pair()
            for half in range(2):
                for ci in range(8):
                    P.pe(lambda e, half=half, ci=ci, co=co, pp_=pp_: e.matmul(
                        PP[pp_][:, half * 512:half * 512 + 512], lhsT=wpw[:, ci, co * 128:(co + 1) * 128],
                        rhs=mixT[:, 8 + ci, half * 512:half * 512 + 512], start=(ci == 0), stop=(ci == 7)),
                        R=[wpw_r, mix_r[8 + ci]], W=[preg[2 * pp_ + half]])
            P.dve(lambda e, co=co, pp_=pp_: e.scalar_tensor_tensor(
                out=mixT[:, co, :], in0=PP[pp_][:], scalar=vec[:, 3, co:co + 1], in1=sga[:, co, :],
                op0=ALU.add, op1=ALU.mult), R=pr(pp_) + [sga_r[co], const_r], W=[mix_r[co]])

    runB = {}

    def stage_attn(blk):
        e0 = blk * TB
        A.reset()
        NVT = 53
        qTn = [A.arena(f"qTn{i}", [128, TB], BF16) for i in range(2)]
        q4b = [A.arena(f"q4b{i}", [128, 4, 256], BF16) for i in range(2)]
        q16b = [A.arena(f"q16b{i}", [128, 17, 64], BF16) for i in range(2)]
        sgbb = [A.arena(f"sgb{i}", [128, TB], BF16) for i in range(3)]
        runB["start"] = A.mark()
        qsq, qsq_r = A.arena("qsq", [128, 512], BF16)
        qraw, qraw_r = A.arena("qraw", [128, 512], F32)
        qln, qln_r = A.arena("qln", [128, 512], F32)
        bth, bth_r = A.arena("bth", [128, 512], F32)
        bhg, bhg_r = A.arena("bhg", [128, 512], F32)
        kTw = [A.arena(f"kTw{i}", [128, 3 * TB], BF16) for i in range(2)]
        vt = [A.arena(f"vt{i}", [128, NVT, 2, 65], BF16) for i in range(2)]
        pt = [A.arena(f"pt{i}", [128, 1024], BF16) for i in range(3)]
        runB["end"] = A.mark()
        acc = [A.arena(f"acc{i}", [65, TB], F32) for i in range(2)]
        dnos = [A.arena(f"dno{i}", [128, TB], F32) for i in range(2)]
        lo = e0 - 2 * TB
        vsR = [vs_r[b] for b in range(max(blk - 2, 0), blk + 1)]
        P.fence()
        BQ, BN = 6, 7

        vparts = [[Reg(f"vt{i}_{k}", arena=True) for k in range(7)] for i in range(2)]

        def load_kv(hp):
            i = hp % 2
            (k_t, k_r), (v_t, _v) = kTw[i], vt[i]
            vr = vparts[i]
            P.dma("sp", k_t[:], kT_s[hp, :, lo:e0 + TB], R=[kTs_r[hp][b] for b in range(blk - 2, blk + 1)], W=[k_r])
            hs = slice(2 * hp, 2 * hp + 2)
            P.dma("sp", v_t[:, 0:9], v_s[e0 - 128:e0 + TB, hs, :].rearrange("(t p) h e -> p t h e", p=128),
                  R=vsR, W=[vr[0]])
            for r in range(4):
                P.dma("sp", v_t[:, 9 + 3 * r:12 + 3 * r],
                      v_s[sl(e0 - 512 + r, 384, 4), hs, :].rearrange("(m p) h e -> p m h e", p=128), R=vsR, W=[vr[1 + r]])
            P.dma("sp", v_t[:, 21:37], v_s[lo:e0, hs, :].rearrange("(p r) h e -> p r h e", r=16), R=vsR, W=[vr[5]])
            P.dma("sp", v_t[:, 37:53], v_s[e0 - TB:e0 + TB, hs, :].rearrange("(p r) h e -> p r h e", r=16),
                  R=vsR, W=[vr[6]])

        wts = {}

        def get_w(kind, hp):
            key = (kind, hp // 4)
            if key not in wts:
                c0 = COL[kind] + (hp // 4) * 512
                wts[key] = load_w(w_in[:, c0:c0 + 512])
            return wts[key]

        def prep_gen(hp):
            i = hp % 2
            j = hp % 4
            (qn_t, qn_r), (q4_t, q4_r), (q16_t, q16_r), (sg_t, sg_r) = qTn[i], q4b[i], q16b[i], sgbb[hp % 3]

            half_box = [0]

            def mm4(kind, half, c0):
                wb, w_r = get_w(kind, hp)
                for c in range(c0, c0 + 4):
                    P.pe(lambda e, c=c: e.matmul(bank_ap(BQ), lhsT=wb[:, c, j * 128:(j + 1) * 128],
                                                 rhs=hT[:, c, half * 512:half * 512 + 512],
                                                 start=(c == 0), stop=(c == 15)),
                         R=[w_r] + hT_r[4 * half:4 * half + 4], W=[preg[BQ]])

            def q_evac():
                P.act(lambda e: e.activation(out=qsq[:], in_=bank_ap(BQ), func=AF.Square), R=[preg[BQ]], W=[qsq_r])
                P.act(lambda e: e.copy(out=qraw[:], in_=bank_ap(BQ)), R=[preg[BQ]], W=[qraw_r])

            def q_norm(half):
                P.pe(lambda e: e.matmul(bank_ap(BN), lhsT=onesbd[:], rhs=qsq[:], start=True, stop=True),
                     R=[qsq_r, const_r], W=[preg[BN]])
                P.act(lambda e: e.activation(out=qln[:], in_=bank_ap(BN), func=AF.Ln, bias=RMS_EPS, scale=1.0 / 64),
                      R=[preg[BN]], W=[qln_r])
                P.act(lambda e: e.activation(out=qln[:], in_=qln[:], func=AF.Exp, scale=-0.5), R=[qln_r], W=[qln_r])
                P.dve(lambda e: e.scalar_tensor_tensor(out=qn_t[:, half * 512:half * 512 + 512], in0=qraw[:],
                                                       scalar=qkg[:, 0:1], in1=qln[:], op0=ALU.mult, op1=ALU.mult),
                      R=[qraw_r, qln_r, const_r], W=[qn_r])

            def g_evac():
                P.act(lambda e: e.activation(out=bth[:], in_=bank_ap(BQ), func=AF.Exp, scale=-1.0),
                      R=[preg[BQ]], W=[bth_r])
                P.act(lambda e: e.activation(out=bth[:], in_=bth[:], func=AF.Ln, bias=1.0), R=[bth_r], W=[bth_r])
                P.act(lambda e: e.activation(out=bth[:], in_=bth[:], func=AF.Exp, scale=-1.0), R=[bth_r], W=[bth_r])
                P.dve(lambda e, half=half_box[0]: e.tensor_tensor(out=sg_t[:, half * 512:half * 512 + 512],
                                                                 in0=bank_ap(BQ), in1=bth[:], op=ALU.mult),
                      R=[preg[BQ], bth_r], W=[sg_r])

            def g_comb(half):
                pass

            def perm_atoms():
                at = []
                for r in range(4):
                    at.append(lambda r=r: P.act(lambda e: e.copy(
                        out=q4_t[:, r, :], in_=qn_t[:].rearrange("p (j r) -> p r j", r=4)[:, r, :]),
                        R=[qn_r], W=[q4_r]))
                for g in range(4):
                    at.append(lambda g=g: P.act(lambda e: e.copy(
                        out=q16_t[:, 4 * g:4 * g + 4, :],
                        in_=qn_t[:].rearrange("p (j r) -> p r j", r=16)[:, 4 * g:4 * g + 4, :]),
                        R=[qn_r], W=[q16_r]))
                at.append(lambda: P.act(lambda e: e.copy(
                    out=q16_t[:, 16:17, :], in_=qn_t[:, 0:64].rearrange("p (r j) -> p r j", r=1)),
                    R=[qn_r], W=[q16_r]))
                return at

            def mm2(kind, half, c0):
                wb, w_r = get_w(kind, hp)
                for c in range(c0, c0 + 2):
                    P.pe(lambda e, c=c: e.matmul(bank_ap(BQ), lhsT=wb[:, c, j * 128:(j + 1) * 128],
                                                 rhs=hT[:, c, half * 512:half * 512 + 512],
                                                 start=(c == 0), stop=(c == 15)),
                         R=[w_r] + hT_r[4 * half:4 * half + 4], W=[preg[BQ]])

            for half in range(2):
                hs_ = slice(half * 512, half * 512 + 512)
                for c0 in range(0, 16, 2):
                    mm2("q", half, c0)
                    yield
                P.act(lambda e: e.activation(out=qsq[:], in_=bank_ap(BQ), func=AF.Square), R=[preg[BQ]], W=[qsq_r])
                yield
                P.act(lambda e: e.copy(out=qraw[:], in_=bank_ap(BQ)), R=[preg[BQ]], W=[qraw_r])
                yield
                side = [
                    lambda: P.pe(lambda e: e.matmul(bank_ap(BN), lhsT=onesbd[:], rhs=qsq[:], start=True, stop=True),
                                 R=[qsq_r, const_r], W=[preg[BN]]),
                    lambda: P.act(lambda e: e.activation(out=qln[:], in_=bank_ap(BN), func=AF.Ln, bias=RMS_EPS,
                                                         scale=1.0 / 64), R=[preg[BN]], W=[qln_r]),
                    lambda: P.act(lambda e: e.activation(out=qln[:], in_=qln[:], func=AF.Exp, scale=-0.5),
                                  R=[qln_r], W=[qln_r]),
                    lambda hs_=hs_: P.dve(lambda e: e.scalar_tensor_tensor(
                        out=qn_t[:, hs_], in0=qraw[:], scalar=qkg[:, 0:1], in1=qln[:], op0=ALU.mult, op1=ALU.mult),
                        R=[qraw_r, qln_r, const_r], W=[qn_r]),
                ]
                pat = perm_atoms() if half == 1 else []
                side += pat[:4]
                for n_, c0 in enumerate(range(0, 16, 2)):
                    mm2("bg", half, c0)
                    yield
                    if n_ < len(side) and side[n_] is not None:
                        side[n_]()
                        yield
                P.act(lambda e: e.activation(out=bth[:], in_=bank_ap(BQ), func=AF.Exp, scale=-1.0),
                      R=[preg[BQ]], W=[bth_r])
                yield
                P.act(lambda e: e.activation(out=bth[:], in_=bth[:], func=AF.Ln, bias=1.0), R=[bth_r], W=[bth_r])
                yield
                P.act(lambda e: e.activation(out=bth[:], in_=bth[:], func=AF.Exp, scale=-1.0), R=[bth_r], W=[bth_r])
                yield
                P.dve(lambda e, hs_=hs_: e.tensor_tensor(out=sg_t[:, hs_], in0=bank_ap(BQ), in1=bth[:], op=ALU.mult),
                      R=[preg[BQ], bth_r], W=[sg_r])
                yield
                for atom in pat[4:]:
                    atom()
                    yield

        def run_all(gen):
            for _ in gen:
                pass

        def sbank():
            ctr["sb"] += 1
            return ctr["sb"] % 4

        def obank():
            ctr["ob"] += 1
            return 4 + ctr["ob"] % 2

        groups = []

        def add_call(hp, h, units, QN, mask, evac, half=False):
            per_s = 512 // (2 * QN)
            call = dict(ob=None)
            ng = len(units) // per_s
            for gi in range(ng):
                groups.append(dict(hp=hp, h=h, QN=QN, mask=mask, call=call, g0=gi * per_s, half=half,
                                   units=units[gi * per_s:(gi + 1) * per_s], evac=evac if gi == ng - 1 else None,
                                   fin=None))

        for hp in range(8):
            for h in range(2):
                a_t, a_r = acc[h]
                for half in range(2):
                    units = [(("n", qt * 128), TB * 2 - 128 + qt * 128, TB * 2 + qt * 128, 1, qt, qt + 1)
                             for qt in range(4 * half, 4 * half + 4)]
                    add_call(hp, h, units, 128, mask128,
                             lambda ob, half=half, a_t=a_t, a_r=a_r: P.act(
                                 lambda e: e.copy(out=a_t[:, half * 512:half * 512 + 512], in_=bank_ap(ob)[0:65, :]),
                                 R=[preg[ob]], W=[a_r]))
                for m in range(2):
                    units = [(("4", r * 256 + 128 * m), 2 * TB + r + 512 * (m - 1), 2 * TB + r + 512 * m, 4,
                              9 + 3 * r + m, 9 + 3 * r + m + 1) for r in range(4)]

                    def ev4(ob, m=m, a_t=a_t, a_r=a_r):
                        av = a_t[:, m * 512:m * 512 + 512].rearrange("p (j r) -> p r j", r=4)
                        P.dve(lambda e: e.tensor_tensor(
                            out=av, in0=bank_ap(ob)[0:65, :].rearrange("p (r j) -> p r j", r=4), in1=av, op=ALU.add),
                            R=[preg[ob], a_r], W=[a_r])
                    add_call(hp, h, units, 128, mask128, ev4)
                for g in range(4):
                    units = [(("16", r * 64), r, TB + r, 16, 21 + r, 37 + r) for r in range(4 * g, 4 * g + 4)]

                    def ev16(ob, g=g, a_t=a_t, a_r=a_r):
                        av = a_t[:, :].rearrange("p (j r) -> p r j", r=16)[:, 4 * g:4 * g + 4, :]
                        src = bank_ap(ob)[0:65, :].rearrange("p (r jj) -> p r jj", r=4)[:, :, 0:64]
                        P.dve(lambda e: e.tensor_tensor(out=av, in0=src, in1=av, op=ALU.add),
                              R=[preg[ob], a_r], W=[a_r])
                    add_call(hp, h, units, 128, mask16, ev16, half=True)
                groups[-1]["fin"] = (hp, h)
        for gi_, G_ in enumerate(groups):
            G_["gi"] = gi_

        def finish_head_dma(hp, h):
            a_t, a_r = acc[h]
            ctr["ds"] += 1
            di = ctr["ds"] % 4
            dno, dno_r = dnos[h]
            P.dma("sp", d_s[di], a_t[64:65, :], R=[a_r], W=[ds_r[di]])
            P.dma("sp", dno[0:64, :], d_s[di].partition_broadcast(64), R=[ds_r[di]], W=[dno_r])

        def finish_head_atoms(hp, h):
            a_t, a_r = acc[h]
            hb = 64 * h
            sg_t, sg_r = sgbb[hp % 3]
            dno, dno_r = dnos[h]
            atoms = []
            for hf in range(2):
                cs = slice(hf * 512, hf * 512 + 512)
                atoms.append(lambda cs=cs: P.act(lambda e: e.activation(out=dno[0:64, cs], in_=dno[0:64, cs], func=AF.Ln),
                                                 R=[dno_r], W=[dno_r]))
                atoms.append(lambda cs=cs: P.act(lambda e: e.activation(out=dno[0:64, cs], in_=dno[0:64, cs], func=AF.Exp,
                                                                        scale=-1.0), R=[dno_r], W=[dno_r]))
            for hf in range(2):
                cs = slice(hf * 512, hf * 512 + 512)
                atoms.append(lambda cs=cs: P.dve(lambda e: e.tensor_tensor(
                    out=dno[hb:hb + 64, cs], in0=a_t[0:64, cs], in1=dno[0:64, cs], op=ALU.mult),
                    R=[a_r, dno_r], W=[dno_r]))
                atoms.append(lambda cs=cs: P.dve(lambda e: e.tensor_tensor(
                    out=mixT[hb:hb + 64, 8 + hp, cs], in0=dno[hb:hb + 64, cs], in1=sg_t[hb:hb + 64, cs], op=ALU.mult),
                    R=[dno_r, sg_r], W=[mix_r[8 + hp]]))
            return atoms

        def emit_S(G):
            hp, h, QN = G["hp"], G["h"], G["QN"]
            (k_t, k_r) = kTw[hp % 2]
            hb = 64 * h
            sb = groups.index(G) % 4 if "gi" not in G else G["gi"] % 4
            G["sb"] = sb
            i = hp % 2
            for uu, ((qlay, q0), kA0, kB0, ks, vA, vB) in enumerate(G["units"]):
                base = uu * 2 * QN
                if qlay == "n":
                    rhs, q_r = qTn[i][0][hb:hb + 64, q0:q0 + QN], qTn[i][1]
                elif qlay == "4":
                    rhs, q_r = q4b[i][0][hb:hb + 64].rearrange("p r j -> p (r j)")[:, q0:q0 + QN], q4b[i][1]
                else:
                    rhs, q_r = q16b[i][0][hb:hb + 64].rearrange("p r j -> p (r j)")[:, q0:q0 + QN], q16b[i][1]
                P.pe(lambda e, base=base, rhs=rhs, kA0=kA0, ks=ks: e.matmul(
                    bank_ap(sb)[:, base:base + QN], lhsT=k_t[hb:hb + 64, sl(kA0, 128, ks)], rhs=rhs,
                    start=True, stop=True), R=[k_r, q_r], W=[preg[sb]])
                P.pe(lambda e, base=base, rhs=rhs, kB0=kB0, ks=ks: e.matmul(
                    bank_ap(sb)[:, base + QN:base + 2 * QN], lhsT=k_t[hb:hb + 64, sl(kB0, 128, ks)], rhs=rhs,
                    start=True, stop=True), R=[k_r, q_r], W=[preg[sb]])

        def emit_expmask2(GA, GB):
            sbA, sbB = GA["sb"], GB["sb"]
            assert sbB == sbA + 1 and sbA % 2 == 0 and GA["mask"] is GB["mask"] and GA["half"] == GB["half"]
            mask = GA["mask"]
            ctr["pt"] = ctr.get("pt", 0) + 1
            p_t, p_r = pt[ctr["pt"] % 3]
            GA["pt"] = (p_t[:, 0:512], p_r)
            GB["pt"] = (p_t[:, 512:1024], p_r)
            src = PP[sbA // 2][:, :]
            m2 = mask[:, :].unsqueeze(1).to_broadcast([128, 2, 512])
            if GA["half"]:
                v = lambda ap: ap.rearrange("p (u j) -> p u j", j=128)[:, :, 0:64]
                mv = m2.rearrange("p t (u j) -> p t u j", j=128)[:, :, :, 0:64]
                pv = lambda ap: ap.rearrange("p (t u j) -> p t u j", t=2, j=128)[:, :, :, 0:64]
            else:
                v = lambda ap: ap
                mv = m2
                pv = lambda ap: ap.rearrange("p (t c) -> p t c", t=2)
            P.act(lambda e: e.activation(out=v(p_t[:]), in_=v(src), func=AF.Exp, scale=0.125),
                  R=[preg[sbA], preg[sbB]], W=[p_r])
            P.dve(lambda e: e.tensor_tensor(out=pv(p_t[:]), in0=pv(p_t[:]), in1=mv, op=ALU.mult),
                  R=[p_r, const_r], W=[p_r])

        def emit_PV(G):
            hp, h, QN = G["hp"], G["h"], G["QN"]
            (v_t, _v) = vt[hp % 2]
            vr = vparts[hp % 2]
            call = G["call"]
            if call["ob"] is None:
                call["ob"] = obank()
            ob = call["ob"]
            p_t, p_r = G["pt"]
            for uu, (_q, kA0, kB0, ks, vA, vB) in enumerate(G["units"]):
                base = uu * 2 * QN
                oc = (G["g0"] + uu) * QN
                P.pe(lambda e, oc=oc, base=base, vA=vA: e.matmul(
                    bank_ap(ob)[0:65, oc:oc + QN], lhsT=v_t[:, vA, h, :], rhs=p_t[:, base:base + QN],
                    start=True, stop=False), R=vr + [p_r], W=[preg[ob]])
                P.pe(lambda e, oc=oc, base=base, vB=vB: e.matmul(
                    bank_ap(ob)[0:65, oc:oc + QN], lhsT=v_t[:, vB, h, :], rhs=p_t[:, base + QN:base + 2 * QN],
                    start=False, stop=True), R=vr + [p_r], W=[preg[ob]])

        deferred = []

        def emit_evac(G, it):
            if G["evac"] is not None:
                G["evac"](G["call"]["ob"])
            if G["fin"] is not None:
                fhp, fh = G["fin"]
                finish_head_dma(fhp, fh)
                for k_, atom in enumerate(finish_head_atoms(fhp, fh)):
                    deferred.append((it + 5 + k_ // 2, atom))
                if fh == 1 and fhp + 2 < 8:
                    load_kv(fhp + 2)

        load_kv(0)
        load_kv(1)
        run_all(prep_gen(0))
        n = len(groups)
        GP = 2
        nit = (n + GP - 1) // GP
        IT_HP = 32 // GP
        gen = None
        for it in range(nit + 3):
            if it % IT_HP == 0 and it < nit:
                if gen is not None:
                    run_all(gen)
                nxt = it // IT_HP + 1
                gen = prep_gen(nxt) if nxt < 8 else None
            odd = it % 2 == 1
            if not odd:
                for gi in range(it * GP, it * GP + GP):
                    if 0 <= gi < n:
                        emit_S(groups[gi])
            if 0 <= (it - 1) * GP < n:
                emit_expmask2(groups[(it - 1) * GP], groups[(it - 1) * GP + 1])
            for gi in range((it - 2) * GP, (it - 2) * GP + GP):
                if 0 <= gi < n:
                    emit_PV(groups[gi])
            for gi in range((it - 3) * GP, (it - 3) * GP + GP):
                if 0 <= gi < n:
                    emit_evac(groups[gi], it)
            if gen is not None:
                for _ in range(4):
                    next(gen, None)
            while deferred and deferred[0][0] <= it:
                deferred.pop(0)[1]()
            if odd:
                for gi in range(it * GP, it * GP + GP):
                    if 0 <= gi < n:
                        emit_S(groups[gi])
        for _, fn in deferred:
            fn()
        runB["regs"] = ([qsq_r, qraw_r, qln_r, bth_r, bhg_r] + [r_ for _, r_ in kTw] + [r_ for _, r_ in vt]
                        + [r_ for part in vparts for r_ in part] + [r_ for _, r_ in pt])

    def stage_out(blk, nxt=None):
        A.reset(runB["start"])
        ot = [A.arena(f"ot{i}", [128, 512], F32) for i in range(2)]
        xr = [A.arena(f"xr{i}", [128, 512], F32) for i in range(4)]
        b0 = s0_alloc() if nxt is not None else None
        assert A.mark() <= runB["end"], (A.mark(), runB["end"])
        P.fence_regs(runB["regs"])
        bg = s0_gen(nxt, hT, hT_r, b0) if nxt is not None else None
        PS["split"] = bg is not None
        n = 0

        def ld(cb, tt, n):
            x_t, x_r = xr[n % 4]
            row0 = blk * TB + tt * 128
            P.dma("sp", x_t[:], xe[row0:row0 + 128, cb * 512:cb * 512 + 512], W=[x_r])
        for n0 in range(3):
            ld(n0 // 8, n0 % 8, n0)
        for cb in range(4):
            wb, w_r = load_w(w_out[:, cb * 512:cb * 512 + 512])
            for tt in range(8):
                nn = n + 3
                if nn < 32:
                    ld(nn // 8, nn % 8, nn)
                b = next_bank()
                for c in range(16):
                    P.pe(lambda e, b=b, c=c, tt=tt, wb=wb: e.matmul(
                        bank_ap(b), lhsT=mixT[:, c, tt * 128:(tt + 1) * 128], rhs=wb[:, c, :],
                        start=(c == 0), stop=(c == 15)), R=[w_r, mix_r[c]], W=[preg[b]])
                x_t, x_r = xr[n % 4]
                o_t, o_r = ot[n % 2]
                P.dve(lambda e, b=b, x_t=x_t, o_t=o_t: e.tensor_tensor(out=o_t[:], in0=bank_ap(b), in1=x_t[:], op=ALU.add),
                      R=[preg[b], x_r], W=[o_r])
                row0 = (blk - 2) * TB + tt * 128
                out_ops.append(P.dma("sp", out_d[row0:row0 + 128, cb * 512:cb * 512 + 512], o_t[:], R=[o_r]))
                n += 1
                if bg is not None and (n == 1 or (n >= 16 and n % 2 == 0)):
                    next(bg, None)
        if bg is not None:
            run_all(bg)
        PS["split"] = False

    HA, HB = (hT, hT_r), (hTB, hTB_r)
    A.reset()
    b0 = s0_alloc()
    P.fence()
    g0 = s0_gen(0, *HA, b0, deep=True)
    next(g0)
    load_consts()
    run_all(g0)
    finish_consts()
    A.reset()
    kvb, chb, b0 = kv_alloc(), ch_alloc(), s0_alloc()
    P.fence()
    H.update(t=HA[0], r=HA[1])
    PS["split"] = True
    g = s0_gen(1, *HB, b0)
    stage_kv(0, kvb, bg=g)
    run_all(g)
    H.update(t=HB[0], r=HB[1])
    g = s0_gen(2, *HA, b0)
    stage_kv(1, kvb, bg=g)
    stage_conv_halo(chb, bgen=g)
    run_all(g)
    PS["split"] = False
    H.update(t=HA[0], r=HA[1])
    for blk in (2, 3):
        A.reset()
        kvb = kv_alloc()
        P.fence()
        stage_kv(blk, kvb)
        stage_conv(blk)
        stage_attn(blk)
        stage_out(blk, nxt=3 if blk == 2 else None)
    P.emit(final_wait_ops=out_ops)
    return nc


def _consts():
    bf = ml_dtypes.bfloat16
    a = np.arange(128)[:, None]
    j = np.arange(128)[None, :]
    mA = (a >= j).astype(np.float32)
    mB = (a <= j).astype(np.float32)
    m128 = np.concatenate([mA, mB, mA, mB], axis=1)
    jj = np.arange(128)[None, :]
    real = jj < 64
    mA16 = ((a >= jj) & real).astype(np.float32)
    mB16 = ((a >= 64) & ((a - 64) <= jj) & real).astype(np.float32)
    m16 = np.concatenate([mA16, mB16, mA16, mB16], axis=1)
    onesbd = np.zeros((128, 128), np.float32)
    onesbd[:64, :64] = 1
    onesbd[64:, 64:] = 1
    return dict(mask128=m128.astype(bf), mask16=m16.astype(bf), ident=np.eye(128, dtype=np.float32).astype(bf),
                onesbd=onesbd.astype(bf), onesf=np.ones((128, 128), np.float32).astype(bf))


def kernel(x, norm_g, w_in, conv_w, conv_b, conv_norm_g, conv_norm_b, conv_pw_w, conv_pw_b,
           q_norm_g, k_norm_g, w_out):
    f = lambda a: np.ascontiguousarray(np.asarray(a, dtype=np.float32))
    x = f(x)
    cm = lambda v: np.ascontiguousarray(f(v)[0].reshape(8, 128).T)
    shared = dict(
        gb=np.ascontiguousarray(np.broadcast_to(f(norm_g)[0], (128, D))),
        w_in=f(w_in)[0],
        convw=np.ascontiguousarray(f(conv_w)[0].reshape(31, 8, 128).transpose(2, 1, 0)),
        vec=np.ascontiguousarray(np.stack([cm(conv_b), cm(conv_norm_g), cm(conv_norm_b), cm(conv_pw_b)], axis=1)),
        qkg=np.ascontiguousarray(np.stack([np.tile(f(q_norm_g)[0], 2), np.tile(f(k_norm_g)[0], 2)], axis=1)),
        pw=f(conv_pw_w)[0],
        w_out=f(w_out)[0],
        **_consts(),
    )
    in_maps = []
    for c in range(8):
        b, half = c // 2, c % 2
        main = x[b, half * TPC:(half + 1) * TPC]
        halo = x[b, 0:TPC] if half == 1 else np.zeros((TPC, D), np.float32)
        m = dict(shared)
        m["xe"] = np.ascontiguousarray(np.concatenate([halo, main], axis=0))
        m["flag"] = np.full((128, 1), float(half), np.float32)
        in_maps.append(m)
    nc = build_program()
    res = run_bass_kernel_spmd(nc, in_maps, core_ids=list(range(8)))
    out = np.empty((NB, S, D), np.float32)
    for c in range(8):
        b, half = c // 2, c % 2
        out[b, half * TPC:(half + 1) * TPC] = res.results[c]["out"]
    return out
```

```python
import numpy as np
import ml_dtypes
from contextlib import ExitStack
import concourse.bass as bass
import concourse.mybir as mybir
from concourse.bass_utils import run_bass_kernel_spmd

F32 = mybir.dt.float32
BF16 = mybir.dt.bfloat16
ALU = mybir.AluOpType
AF = mybir.ActivationFunctionType
COMPUTE = ("pe", "act", "dve", "pool")

D = 2048
S = 4096
NB = 4
TPC = 2048
TB = 1024
IN_W = 7168
COL = dict(val=0, glu=1024, gate=2048, q=3072, k=4096, v=5120, bg=6144)
RMS_EPS = 1e-6
LN_EPS = 1e-5
SB_BASE = 16512
SB_END = 229344
NPE = 16
NPE_SCHED = [16, 16, 16, 16, 16, 16, 18, 22]


def sl(start, count, step=1):
    return slice(start, start + (count - 1) * step + 1, step)


class Reg:
    __slots__ = ("name", "w", "r", "arena")

    def __init__(self, name, arena=False):
        self.name = name
        self.w = None
        self.r = []
        self.arena = arena


class Op:
    __slots__ = ("eng", "fn", "deps", "is_dma", "signal", "ticket", "sem", "idx")

    def __init__(self, eng, fn, is_dma):
        self.eng = eng
        self.fn = fn
        self.is_dma = is_dma
        self.deps = []
        self.signal = False
        self.ticket = None
        self.sem = None


class Prog:
    def __init__(self, nc):
        self.nc = nc
        self.ops = []
        self.n_dma_sems = {"sp": 24, "pool": 6, "act": 2}
        self.last_arena = {}
        self.arena_dmas = []
        self.pending = {}

    def fence(self):
        f = list(self.last_arena.values()) + self.arena_dmas
        self.arena_dmas = []
        for e in ("pe", "act", "dve", "pool", "sp"):
            old = [d for d in self.pending.get(e, []) if d.is_dma]
            self.pending[e] = old + f

    def fence_regs(self, regs):
        f = {}
        for r in regs:
            if r.w is not None:
                f[r.w.idx] = r.w
            for rd in r.r:
                f[rd.idx] = rd
        f = list(f.values())
        for e in ("pe", "act", "dve", "pool", "sp"):
            old = [d for d in self.pending.get(e, []) if d.is_dma]
            self.pending[e] = old + f

    def add(self, eng, fn, R=(), W=(), is_dma=False):
        op = Op(eng, fn, is_dma)
        op.idx = len(self.ops)
        deps = {}
        arena = False
        for r in R:
            arena |= r.arena
            if r.w is not None:
                deps[r.w.idx] = r.w
        for w in W:
            arena |= w.arena
            if w.w is not None:
                deps[w.w.idx] = w.w
            for rd in w.r:
                deps[rd.idx] = rd
        if arena:
            for d in self.pending.get(eng, ()):
                deps[d.idx] = d
            self.pending[eng] = []
        best = {}
        for d in deps.values():
            if d.is_dma:
                best[("dma", d.idx)] = d
            else:
                if (not is_dma) and d.eng == eng and eng == "pe":
                    continue
                b = best.get(d.eng)
                if b is None or d.idx > b.idx:
                    best[d.eng] = d
        for d in best.values():
            d.signal = True
            op.deps.append(d)
        for w in W:
            w.w = op
            w.r = []
        for r in R:
            if r.w is not op:
                if is_dma:
                    r.r.append(op)
                else:
                    r.r = [x for x in r.r if x.is_dma or x.eng != eng]
                    r.r.append(op)
        if arena:
            if is_dma:
                self.arena_dmas.append(op)
            else:
                self.last_arena[eng] = op
        self.ops.append(op)
        return op

    def pe(self, fn, R=(), W=()):
        return self.add("pe", fn, R, W)

    def act(self, fn, R=(), W=()):
        return self.add("act", fn, R, W)

    def dve(self, fn, R=(), W=()):
        return self.add("dve", fn, R, W)

    def pool(self, fn, R=(), W=()):
        return self.add("pool", fn, R, W)

    def dma(self, queue, out, in_, R=(), W=(), **kw):
        return self.add(queue, lambda e: e.dma_start(out=out, in_=in_, **kw), R, W, is_dma=True)

    def emit(self, final_wait_ops=()):
        nc = self.nc
        with ExitStack() as es:
            esem = {e: es.enter_context(nc.semaphore("s_" + e)) for e in COMPUTE}
            dsem = {q: [es.enter_context(nc.semaphore(f"d_{q}{i}")) for i in range(n)]
                    for q, n in self.n_dma_sems.items()}
            ecount = {e: 0 for e in COMPUTE}
            dcount = {q: [0] * n for q, n in self.n_dma_sems.items()}
            drr = {q: 0 for q in self.n_dma_sems}
            dprev = {}
            for op in self.ops:
                if op.is_dma:
                    q = op.eng
                    j = drr[q] % self.n_dma_sems[q]
                    drr[q] += 1
                    op.sem = dsem[q][j]
                    dprev[op.idx] = dcount[q][j]
                    dcount[q][j] += 16
                    op.ticket = dcount[q][j]
                elif op.signal:
                    ecount[op.eng] += 1
                    op.ticket = ecount[op.eng]
                    op.sem = esem[op.eng]
            by_eng = {}
            for op in self.ops:
                by_eng.setdefault(op.eng, []).append(op)
            final_wait_ops = list(final_wait_ops)

            def run(engname, e):
                waited = {}

                def wait(sem, val):
                    k = id(sem)
                    if waited.get(k, 0) >= val:
                        return
                    waited[k] = val
                    e.wait_ge(sem, val)

                for op in by_eng.get(engname, []):
                    for d in op.deps:
                        wait(d.sem, d.ticket)
                    if op.is_dma:
                        if dprev[op.idx] > 0:
                            wait(op.sem, dprev[op.idx])
                        op.fn(e).then_inc(op.sem, 16)
                    else:
                        ins = op.fn(e)
                        if op.signal:
                            ins.then_inc(op.sem, 1)
                if engname == "sp":
                    for op in final_wait_ops:
                        wait(op.sem, op.ticket)

            with nc.Block() as block:
                @block.sync
                def _(e):
                    run("sp", e)

                @block.tensor
                def _(e):
                    run("pe", e)

                @block.scalar
                def _(e):
                    run("act", e)

                @block.vector
                def _(e):
                    run("dve", e)

                @block.gpsimd
                def _(e):
                    run("pool", e)


class Alloc:
    def __init__(self, nc):
        self.nc = nc
        self.pp = SB_BASE
        self.abase = None
        self.ap_ = None
        self.ctr = 0
        self.peak = 0

    def _mk(self, name, shape, dt, off):
        self.ctr += 1
        return self.nc.alloc_sbuf_tensor_at(f"{name}_{self.ctr}", list(shape), dt, offset=off)

    @staticmethod
    def _bytes(shape, dt):
        n = 1
        for s in shape[1:]:
            n *= s
        n *= 2 if dt == BF16 else 4
        return (n + 31) // 32 * 32

    def persist(self, name, shape, dt):
        assert self.abase is None
        t = self._mk(name, shape, dt, self.pp)
        self.pp += self._bytes(shape, dt)
        return t

    def start_arena(self):
        self.abase = self.pp
        self.ap_ = self.pp

    def reset(self, to=None):
        self.ap_ = self.abase if to is None else to

    def mark(self):
        return self.ap_

    def arena(self, name, shape, dt, reg=True):
        t = self._mk(name, shape, dt, self.ap_)
        self.ap_ += self._bytes(shape, dt)
        self.peak = max(self.peak, self.ap_)
        assert self.ap_ <= SB_END, (name, self.ap_, SB_END)
        return (t, Reg(name, arena=True)) if reg else t


def build_program():
    nc = bass.Bass("TRN2", target_bir_lowering=False)
    dt_in = lambda name, shape, dt=F32: nc.dram_tensor(name, list(shape), dt, kind="ExternalInput").ap()
    xe = dt_in("xe", [2 * TPC, D])
    gb = dt_in("gb", [128, D])
    w_in = dt_in("w_in", [D, IN_W])
    convw_d = dt_in("convw", [128, 8, 31])
    vec_d = dt_in("vec", [128, 4, 8])
    qkg_d = dt_in("qkg", [128, 2])
    pw_d = dt_in("pw", [1024, 1024])
    w_out = dt_in("w_out", [D, D])
    flag_d = dt_in("flag", [128, 1])
    mask128_d = dt_in("mask128", [128, 512], BF16)
    mask16_d = dt_in("mask16", [128, 512], BF16)
    ident_d = dt_in("ident", [128, 128], BF16)
    onesbd_d = dt_in("onesbd", [128, 128], BF16)
    onesf_d = dt_in("onesf", [128, 128], BF16)
    out_d = nc.dram_tensor("out", [TPC, D], F32, kind="ExternalOutput").ap()
    kT_s = nc.dram_tensor("kT_s", [8, 128, 2 * TPC], BF16, kind="Internal").ap()
    v_s = nc.dram_tensor("v_s", [2 * TPC, 16, 65], BF16, kind="Internal").ap()
    d_s = nc.dram_tensor("d_s", [4, 1, TB], F32, kind="Internal").ap()

    P = Prog(nc)
    A = Alloc(nc)

    hT = A.persist("hT", [128, 16, TB], BF16)
    hT_r = [Reg(f"hT{t}") for t in range(8)]
    hT_off = SB_BASE
    wbuf, wbuf_pw, wreg = [], [], []
    for i in range(3):
        off = A.pp
        wbuf.append(A.persist(f"wbuf{i}", [128, 16, 512], BF16))
        wbuf_pw.append(A._mk(f"wbufpw{i}", [128, 8, 1024], BF16, off))
        wreg.append(Reg(f"wbuf{i}"))
    mix_off = A.pp
    mixT = A.persist("mixT", [128, 16, TB], BF16)
    mix_r = [Reg(f"mix{c}") for c in range(16)]
    hTB = A._mk("hTB", [128, 16, TB], BF16, mix_off)
    hTB_r = [Reg(f"hTB{t}", arena=True) for t in range(8)]
    H = dict(t=hT, r=hT_r)
    ident = A.persist("ident", [128, 128], BF16)
    onesbd = A.persist("onesbd", [128, 128], BF16)
    onesf = A.persist("onesf", [128, 128], BF16)
    mask128 = A.persist("mask128", [128, 512], BF16)
    mask16 = A.persist("mask16", [128, 512], BF16)
    convw = A.persist("convw", [128, 8, 31], F32)
    vec = A.persist("vec", [128, 4, 8], F32)
    vec05 = A.persist("vec05", [128, 2, 8], F32)
    qkg = A.persist("qkg", [128, 2], F32)
    flag = A.persist("flag", [128, 1], F32)
    u_tail = A.persist("u_tail", [128, 8, 30], F32)
    utail_r = [Reg(f"utail{c}") for c in range(8)]
    const_r = Reg("consts")
    A.start_arena()

    PP = [nc.alloc_psum_tensor(f"PP{i}", [128, 1024], F32) for i in range(4)]
    PPb = [t.bitcast(BF16) for t in PP]
    preg = [Reg(f"bank{i}") for i in range(8)]

    def bank_ap(b):
        return PP[b // 2][:, (b % 2) * 512:(b % 2) * 512 + 512]

    def bank_bf(b):
        return PPb[b // 2][:, (b % 2) * 1024:(b % 2) * 1024 + 1024].rearrange("p (c t) -> p c t", c=8)

    ctr = dict(pair=0, bank=0, w=0, sb=0, ob=0, ds=0)

    PS = dict(split=False)

    def next_pair():
        ctr["pair"] += 1
        return ctr["pair"] % (3 if PS["split"] else 4)

    def next_bank():
        ctr["bank"] += 1
        return ctr["bank"] % (6 if PS["split"] else 8)

    def s0_bank():
        ctr["s0b"] = ctr.get("s0b", 0) + 1
        return 6 + ctr["s0b"] % 2

    def load_w(src, pw=False):
        i = ctr["w"] % 3
        ctr["w"] += 1
        dst = wbuf_pw[i] if pw else wbuf[i]
        P.dma("pool", dst[:], src.rearrange("(c p) n -> p c n", p=128), W=[wreg[i]])
        return dst, wreg[i]

    cregs = []
    ident_r = Reg("ident")
    P.dma("sp", ident[:], ident_d, W=[ident_r])
    rest_consts = ((onesbd, onesbd_d), (onesf, onesf_d), (mask128, mask128_d),
                   (mask16, mask16_d), (convw, convw_d), (vec, vec_d), (qkg, qkg_d), (flag, flag_d))

    def load_consts():
        for dst, src in rest_consts:
            cregs.append(Reg("c%d" % len(cregs)))
            P.dma("sp", dst[:], src, W=[cregs[-1]])

    def finish_consts():
        P.dve(lambda e: e.tensor_scalar(out=vec05[:], in0=vec[:, 1:3, :], scalar1=0.5, scalar2=None, op0=ALU.mult),
              R=cregs + [ident_r], W=[const_r])

    kTs_r = [[Reg(f"kTs{hp}_{b}") for b in range(NB)] for hp in range(8)]
    vs_r = [Reg(f"vs{b}") for b in range(NB)]
    ds_r = [Reg(f"ds{i}") for i in range(4)]
    out_ops = []

    def s0_alloc():
        return dict(
            xt=[A.arena(f"xt{i}", [128, D], F32) for i in range(3)],
            xn=[A.arena(f"xn{i}", [128, D], BF16) for i in range(2)],
            gbt=A.arena("gbt", [128, D], F32),
            ss=[A.arena(f"ss{i}", [128, 1], F32) for i in range(2)],
            rs=[A.arena(f"rs{i}", [128, 1], F32) for i in range(2)])

    def s0_gen(blk, hT, hT_r, bufs, deep=False):
        xt, xn, ss, rs = bufs["xt"], bufs["xn"], bufs["ss"], bufs["rs"]
        gbt, gbt_r = bufs["gbt"]
        P.dma("sp", gbt[:], gb, W=[gbt_r])
        banks = {}

        def stL(tt):
            (x_t, x_r) = xt[tt % 3]
            row0 = blk * TB + tt * 128
            P.dma("sp", x_t[:], xe[row0:row0 + 128, :], W=[x_r])

        def stA(tt):
            i = tt % 2
            (x_t, x_r), (s_t, s_r), (r_t, r_r) = xt[tt % 3], ss[i], rs[i]
            (j_t, j_r) = xn[i]
            P.act(lambda e: e.activation(out=j_t[:], in_=x_t[:], func=AF.Square, accum_out=s_t[:]),
                  R=[x_r], W=[j_r, s_r])
            P.act(lambda e: e.activation(out=r_t[:], in_=s_t[:], func=AF.Ln, bias=RMS_EPS, scale=1.0 / D),
                  R=[s_r], W=[r_r])
            P.act(lambda e: e.activation(out=r_t[:], in_=r_t[:], func=AF.Exp, scale=-0.5), R=[r_r], W=[r_r])

        def stB(tt):
            i = tt % 2
            (x_t, x_r), (n_t, n_r), (r_t, r_r) = xt[tt % 3], xn[i], rs[i]
            P.dve(lambda e: e.scalar_tensor_tensor(
                out=n_t[:], in0=x_t[:], scalar=r_t[:], in1=gbt[:], op0=ALU.mult, op1=ALU.mult),
                R=[x_r, r_r, gbt_r], W=[n_r])
            banks[tt] = []
            for half in range(2):
                b = s0_bank()
                banks[tt].append(b)
                for c8 in range(8):
                    c = half * 8 + c8
                    P.pe(lambda e, b=b, c8=c8, c=c: e.transpose(
                        out=bank_bf(b)[:, c8, :], in_=n_t[:, c * 128:(c + 1) * 128], identity=ident[:]),
                        R=[n_r, ident_r], W=[preg[b]])

        def stC(tt):
            for half in range(2):
                b = banks[tt][half]
                dst = hT[:, half * 8:half * 8 + 8, tt * 128:(tt + 1) * 128]
                if half == 0:
                    P.act(lambda e, b=b, dst=dst: e.copy(out=dst, in_=bank_bf(b)), R=[preg[b]], W=[hT_r[tt]])
                else:
                    P.dve(lambda e, b=b, dst=dst: e.tensor_copy(out=dst, in_=bank_bf(b)), R=[preg[b]], W=[hT_r[tt]])

        stL(0)
        if deep:
            stL(1)
        for it in range(8 + 1):
            if not deep and it + 1 < 8:
                stL(it + 1)
            if it < 8:
                stA(it)
            if 0 <= it - 1 < 8:
                stB(it - 1)
                stC(it - 1)
            if deep and it + 2 < 8:
                stL(it + 2)
            yield

    def run_all(gen):
        for _ in gen:
            pass

    def proj_fm(wb, w_r, j, p=None):
        if p is None:
            p = next_pair()
        hT, hT_r = H["t"], H["r"]
        for half in range(2):
            for c in range(16):
                P.pe(lambda e, p=p, half=half, c=c: e.matmul(
                    PP[p][:, half * 512:half * 512 + 512], lhsT=wb[:, c, j * 128:(j + 1) * 128],
                    rhs=hT[:, c, half * 512:half * 512 + 512], start=(c == 0), stop=(c == 15)),
                    R=[w_r] + hT_r[4 * half:4 * half + 4], W=[preg[2 * p + half]])
        return p

    def pr(p):
        return [preg[2 * p], preg[2 * p + 1]]

    def head_norm(p, gcol, tmp, dst, dst_regs, p2=None):
        (sq_t, sq_r), (ln_t, ln_r) = tmp
        P.act(lambda e: e.activation(out=sq_t[:], in_=PP[p][:], func=AF.Square), R=pr(p), W=[sq_r])
        if p2 is None:
            p2 = next_pair()
        for half in range(2):
            P.pe(lambda e, half=half: e.matmul(PP[p2][:, half * 512:half * 512 + 512], lhsT=onesbd[:],
                                               rhs=sq_t[:, half * 512:half * 512 + 512], start=True, stop=True),
                 R=[sq_r, const_r], W=[preg[2 * p2 + half]])
        P.act(lambda e: e.activation(out=ln_t[:], in_=PP[p2][:], func=AF.Ln, bias=RMS_EPS, scale=1.0 / 64),
              R=pr(p2), W=[ln_r])
        P.act(lambda e: e.activation(out=ln_t[:], in_=ln_t[:], func=AF.Exp, scale=-0.5), R=[ln_r], W=[ln_r])
        P.dve(lambda e: e.scalar_tensor_tensor(out=dst, in0=PP[p][:], scalar=qkg[:, gcol:gcol + 1], in1=ln_t[:],
                                               op0=ALU.mult, op1=ALU.mult),
              R=pr(p) + [ln_r, const_r], W=dst_regs)

    def kv_alloc():
        return dict(vnat=A.arena("vnat", [128, 8, 16, 65], BF16),
                    vregs=[Reg(f"vnat{t}", arena=True) for t in range(8)],
                    tmps=[(A.arena(f"ksq{i}", [128, TB], BF16), A.arena(f"kln{i}", [128, TB], F32)) for i in range(2)],
                    kTo=[A.arena(f"kTo{i}", [128, TB], BF16) for i in range(2)])

    def stage_kv(blk, bufs, bg=None):
        vnat, vnat_r = bufs["vnat"]
        vregs = bufs["vregs"]
        tmps, kTo = bufs["tmps"], bufs["kTo"]
        hT, hT_r = H["t"], H["r"]
        P.dve(lambda e: e.memset(vnat[:, :, :, 64:65], 1.0), W=vregs)
        if blk < 2:
            P.dve(lambda e: e.tensor_scalar(out=vnat[:, :, :, 64:65], in0=vnat[:, :, :, 64:65], scalar1=flag[:, 0:1],
                                            scalar2=None, op0=ALU.mult), R=[const_r], W=vregs)
        for cb in range(2):
            wb, w_r = load_w(w_in[:, COL["v"] + cb * 512:COL["v"] + cb * 512 + 512])
            for tt in range(8):
                b = next_bank()
                for c in range(16):
                    P.pe(lambda e, b=b, c=c, tt=tt, wb=wb: e.matmul(
                        bank_ap(b), lhsT=hT[:, c, tt * 128:(tt + 1) * 128], rhs=wb[:, c, :],
                        start=(c == 0), stop=(c == 15)), R=[w_r, hT_r[tt]], W=[preg[b]])
                dst = vnat[:, tt, cb * 8:cb * 8 + 8, 0:64]
                src = lambda b=b: bank_ap(b).rearrange("p (h e) -> p h e", e=64)
                if tt % 2 == 0:
                    P.act(lambda e, dst=dst, src=src: e.copy(out=dst, in_=src()), R=[preg[b]], W=[vregs[tt]])
                else:
                    P.dve(lambda e, dst=dst, src=src: e.tensor_copy(out=dst, in_=src()), R=[preg[b]], W=[vregs[tt]])
                if bg is not None and tt % 4 == 3:
                    next(bg, None)
        P.dma("sp", v_s[blk * TB:(blk + 1) * TB].rearrange("(t p) h e -> p t (h e)", p=128),
              vnat[:].rearrange("p t h e -> p t (h e)"), R=vregs, W=[vs_r[blk]])
        wk = [load_w(w_in[:, COL["k"] + wi * 512:COL["k"] + wi * 512 + 512]) for wi in range(2)]
        p_next = proj_fm(wk[0][0], wk[0][1], 0)
        for hp in range(8):
            p = p_next
            pipelined = not PS["split"]
            if hp + 1 < 8 and pipelined:
                wb, w_r = wk[(hp + 1) // 4]
                p_next = proj_fm(wb, w_r, (hp + 1) % 4)
            ko, ko_r = kTo[hp % 2]
            head_norm(p, 1, tmps[hp % 2], ko[:], [ko_r])
            if hp + 1 < 8 and not pipelined:
                wb, w_r = wk[(hp + 1) // 4]
                p_next = proj_fm(wb, w_r, (hp + 1) % 4)
            P.dma("sp", kT_s[hp, :, blk * TB:(blk + 1) * TB], ko[:], R=[ko_r], W=[kTs_r[hp][blk]])
            if bg is not None:
                next(bg, None)

    def act_gate(pg, th, hg=None):
        (th_t, th_r) = th
        P.act(lambda e: e.activation(out=th_t[:], in_=PP[pg][:], func=AF.Tanh, scale=0.5), R=pr(pg), W=[th_r])
        if hg is not None:
            (hg_t, hg_r) = hg
            P.act(lambda e: e.activation(out=hg_t[:], in_=PP[pg][:], func=AF.Copy, scale=0.5), R=pr(pg), W=[hg_r])

    def ch_alloc():
        return dict(th=A.arena("th", [128, 128], F32), hv=A.arena("hv", [128, 128], F32))

    def stage_conv_halo(bufs, bgen=None):
        th, th_r = bufs["th"]
        hv, hv_r = bufs["hv"]
        hT, hT_r = H["t"], H["r"]
        for wi in range(2):
            wv, wv_r = load_w(w_in[:, COL["val"] + wi * 512:COL["val"] + wi * 512 + 512])
            wg, wg_r = load_w(w_in[:, COL["glu"] + wi * 512:COL["glu"] + wi * 512 + 512])
            for j in range(4):
                c = wi * 4 + j
                bv, bg = next_bank(), next_bank()
                for (b, wb, w_r) in ((bv, wv, wv_r), (bg, wg, wg_r)):
                    for ck in range(16):
                        P.pe(lambda e, b=b, wb=wb, ck=ck, j=j: e.matmul(
                            bank_ap(b)[:, 0:128], lhsT=wb[:, ck, j * 128:(j + 1) * 128], rhs=hT[:, ck, 896:1024],
                            start=(ck == 0), stop=(ck == 15)), R=[w_r, hT_r[7]], W=[preg[b]])
                P.act(lambda e, bg=bg: e.activation(out=th[:], in_=bank_ap(bg)[:, 0:128], func=AF.Tanh, scale=0.5),
                      R=[preg[bg]], W=[th_r])
                P.act(lambda e, bv=bv: e.activation(out=hv[:], in_=bank_ap(bv)[:, 0:128], func=AF.Copy, scale=0.5),
                      R=[preg[bv]], W=[hv_r])
                P.dve(lambda e, c=c: e.scalar_tensor_tensor(out=u_tail[:, c, :], in0=th[:, 98:128], scalar=1.0,
                                                            in1=hv[:, 98:128], op0=ALU.add, op1=ALU.mult),
                      R=[th_r, hv_r], W=[utail_r[c]])
                if bgen is not None and c % 2 == 1:
                    next(bgen, None)

    def stage_conv(blk):
        A.reset()
        y = A.arena("y", [128, 8, TB], F32, reg=False)
        y_r = [Reg(f"y{c}", arena=True) for c in range(8)]
        mk = A.mark()
        u = [A.arena(f"u{i}", [128, TB + 30], F32) for i in range(2)]
        ubf = [A.arena(f"ubf{i}", [128, TB + 32], BF16) for i in range(2)]
        th = [A.arena(f"th{i}", [128, TB], F32) for i in range(2)]
        hv = [A.arena(f"hv{i}", [128, TB], F32) for i in range(2)]
        accA = [A.arena(f"accA{i}", [128, TB], F32) for i in range(2)]
        wd = [A.arena(f"wd{i}", [128, max(NPE_SCHED), 128], BF16) for i in range(2)]
        P.fence()
        tDs = [list(range(0, 31 - n_pe)) for n_pe in NPE_SCHED]
        tPs = [list(range(31 - n_pe, 31)) for n_pe in NPE_SCHED]
        wts = {}

        def load_pair(wi):
            wts[wi] = (load_w(w_in[:, COL["val"] + wi * 512:COL["val"] + wi * 512 + 512]),
                       load_w(w_in[:, COL["glu"] + wi * 512:COL["glu"] + wi * 512 + 512]))

        def proj_fixed(wb, w_r, j, p):
            for half in range(2):
                for ck in range(16):
                    P.pe(lambda e, half=half, ck=ck: e.matmul(
                        PP[p][:, half * 512:half * 512 + 512], lhsT=wb[:, ck, j * 128:(j + 1) * 128],
                        rhs=hT[:, ck, half * 512:half * 512 + 512], start=(ck == 0), stop=(ck == 15)),
                        R=[w_r] + hT_r[4 * half:4 * half + 4], W=[preg[2 * p + half]])

        def proj(c):
            if c // 4 not in wts:
                load_pair(c // 4)
            (wv, wv_r), (wg, wg_r) = wts[c // 4]
            proj_fixed(wv, wv_r, c % 4, 0)
            proj_fixed(wg, wg_r, c % 4, 1)

        def merge_pe(c):
            pc = 2 + c % 2
            yc = y[:, c, :]
            P.dve(lambda e: e.tensor_tensor(out=yc, in0=PP[pc][:], in1=yc, op=ALU.add),
                  R=pr(pc) + [y_r[c]], W=[y_r[c]])

        def build_wd(c):
            wd_t, wd_r = wd[c % 2]
            for n_, k in enumerate(tPs[c]):
                P.act(lambda e, n_=n_, k=k: e.activation(
                    out=wd_t[:, n_, :], in_=ident[:], func=AF.Copy, scale=convw[:, c, k:k + 1]),
                    R=[const_r], W=[wd_r])

        proj(0)
        if NPE:
            build_wd(0)
        for c in range(8):
            i = c % 2
            pv, pg = 0, 1
            (u_t, u_r), (ub_t, ub_r), (th_t, th_r), (hv_t, hv_r), (aa_t, aa_r), (wd_t, wd_r) = \
                u[i], ubf[i], th[i], hv[i], accA[i], wd[i]
            act_gate(pg, th[i])
            P.act(lambda e, hv_t=hv_t: e.activation(out=hv_t[:], in_=PP[pv][:], func=AF.Copy, scale=0.5),
                  R=pr(pv), W=[hv_r])
            if c + 1 < 8:
                proj(c + 1)
            P.dve(lambda e, u_t=u_t, c=c: e.tensor_copy(out=u_t[:, 0:30], in_=u_tail[:, c, :]),
                  R=[utail_r[c]], W=[u_r])
            P.dve(lambda e, u_t=u_t, th_t=th_t, hv_t=hv_t: e.scalar_tensor_tensor(
                out=u_t[:, 30:30 + TB], in0=th_t[:], scalar=1.0, in1=hv_t[:], op0=ALU.add, op1=ALU.mult),
                R=[th_r, hv_r], W=[u_r])
            P.dve(lambda e, u_t=u_t, c=c: e.tensor_copy(out=u_tail[:, c, :], in_=u_t[:, TB:TB + 30]),
                  R=[u_r], W=[utail_r[c]])
            yc = y[:, c, :]
            if NPE:
                P.act(lambda e, u_t=u_t, ub_t=ub_t: e.copy(out=ub_t[:, 0:TB + 30], in_=u_t[:]), R=[u_r], W=[ub_r])
                if c + 1 < 8:
                    build_wd(c + 1)
                pc = 2 + c % 2
                for half in range(2):
                    for n_, k in enumerate(tPs[c]):
                        P.pe(lambda e, half=half, n_=n_, k=k, pc=pc, wd_t=wd_t, ub_t=ub_t, last=len(tPs[c]) - 1: e.matmul(
                            PP[pc][:, half * 512:half * 512 + 512], lhsT=wd_t[:, n_, :],
                            rhs=ub_t[:, k + half * 512:k + half * 512 + 512], start=(n_ == 0), stop=(n_ == last)),
                            R=[wd_r, ub_r], W=[preg[2 * pc + half]])
            tD = tDs[c]
            tA = tD[0::2]
            tA2 = tD[1::2]
            P.dve(lambda e, u_t=u_t, c=c, yc=yc: e.tensor_scalar(
                out=yc, in0=u_t[:, 0:TB], scalar1=convw[:, c, 0:1], scalar2=vec[:, 0, c:c + 1],
                op0=ALU.mult, op1=ALU.add), R=[u_r, const_r], W=[y_r[c]])
            P.dve(lambda e, u_t=u_t, c=c, aa_t=aa_t: e.tensor_scalar(
                out=aa_t[:], in0=u_t[:, 1:1 + TB], scalar1=convw[:, c, 1:2], scalar2=None, op0=ALU.mult),
                R=[u_r, const_r], W=[aa_r])
            for k in tD[2:]:
                if k in tA:
                    P.dve(lambda e, u_t=u_t, c=c, k=k, yc=yc: e.scalar_tensor_tensor(
                        out=yc, in0=u_t[:, k:k + TB], scalar=convw[:, c, k:k + 1], in1=yc,
                        op0=ALU.mult, op1=ALU.add), R=[u_r, const_r, y_r[c]], W=[y_r[c]])
                else:
                    P.dve(lambda e, u_t=u_t, c=c, k=k, aa_t=aa_t: e.scalar_tensor_tensor(
                        out=aa_t[:], in0=u_t[:, k:k + TB], scalar=convw[:, c, k:k + 1], in1=aa_t[:],
                        op0=ALU.mult, op1=ALU.add), R=[u_r, const_r, aa_r], W=[aa_r])
            P.dve(lambda e, yc=yc, aa_t=aa_t: e.tensor_tensor(out=yc, in0=yc, in1=aa_t[:], op=ALU.add),
                  R=[y_r[c], aa_r], W=[y_r[c]])
            if NPE and c >= 1:
                merge_pe(c - 1)
        if NPE:
            merge_pe(7)
        A.reset(mk)
        ybf = [A.arena(f"ybf{i}", [128, TB], BF16) for i in range(2)]
        ysq = [A.arena(f"ysq{i}", [128, TB], BF16) for i in range(2)]
        mean, mean_r = A.arena("mean", [128, TB], F32)
        rstd, rstd_r = A.arena("rstd", [128, TB], F32)
        th2 = [A.arena(f"th2{i}", [128, TB], F32) for i in range(2)]
        vp = [A.arena(f"vp{i}", [128, TB], F32) for i in range(2)]
        tmp, tmp_r = vp[0]
        sga = A.arena("sga", [128, 8, TB], BF16, reg=False)
        sga_r = [Reg(f"sga{c}", arena=True) for c in range(8)]
        gth = A.arena("gth", [128, TB], F32)
        ghg = A.arena("ghg", [128, TB], F32)
        P.fence()
        wg0 = load_w(w_in[:, COL["gate"]:COL["gate"] + 512])
        wg1 = load_w(w_in[:, COL["gate"] + 512:COL["gate"] + 1024])
        wpw, wpw_r = load_w(pw_d, pw=True)
        pS, pQ = next_pair(), next_pair()
        for c in range(8):
            i = c % 2
            (yb_t, yb_r), (ys_t, ys_r) = ybf[i], ysq[i]
            P.act(lambda e, c=c, yb_t=yb_t: e.copy(out=yb_t[:], in_=y[:, c, :]), R=[y_r[c]], W=[yb_r])
            P.act(lambda e, c=c, ys_t=ys_t: e.activation(out=ys_t[:], in_=y[:, c, :], func=AF.Square), R=[y_r[c]], W=[ys_r])
            for (p, t, r_) in ((pS, yb_t, yb_r), (pQ, ys_t, ys_r)):
                for half in range(2):
                    P.pe(lambda e, p=p, t=t, half=half, c=c: e.matmul(
                        PP[p][:, half * 512:half * 512 + 512], lhsT=onesf[:], rhs=t[:, half * 512:half * 512 + 512],
                        start=(c == 0), stop=(c == 7)), R=[r_, const_r], W=[preg[2 * p + half]])
        P.dve(lambda e: e.tensor_scalar(out=mean[:], in0=PP[pS][:], scalar1=1.0 / 1024, scalar2=None, op0=ALU.mult),
              R=pr(pS), W=[mean_r])
        P.dve(lambda e: e.tensor_tensor(out=tmp[:], in0=mean[:], in1=mean[:], op=ALU.mult), R=[mean_r], W=[tmp_r])
        P.dve(lambda e: e.scalar_tensor_tensor(out=tmp[:], in0=PP[pQ][:], scalar=1.0 / 1024, in1=tmp[:],
                                               op0=ALU.mult, op1=ALU.subtract), R=pr(pQ) + [tmp_r], W=[tmp_r])
        P.act(lambda e: e.activation(out=rstd[:], in_=tmp[:], func=AF.Ln, bias=LN_EPS), R=[tmp_r], W=[rstd_r])
        P.act(lambda e: e.activation(out=rstd[:], in_=rstd[:], func=AF.Exp, scale=-0.5), R=[rstd_r], W=[rstd_r])
        for c in range(8):
            i = c % 2
            (t2_t, t2_r), (vp_t, vp_r) = th2[i], vp[i]
            yc = y[:, c, :]
            wg, wg_r = (wg0, wg1)[c // 4]
            pg = proj_fm(wg, wg_r, c % 4)
            P.dve(lambda e, yc=yc: e.tensor_tensor(out=yc, in0=yc, in1=mean[:], op=ALU.subtract),
                  R=[y_r[c], mean_r], W=[y_r[c]])
            P.dve(lambda e, yc=yc: e.tensor_tensor(out=yc, in0=yc, in1=rstd[:], op=ALU.mult),
                  R=[y_r[c], rstd_r], W=[y_r[c]])
            P.act(lambda e, yc=yc, c=c, t2_t=t2_t: e.activation(out=t2_t[:], in_=yc, func=AF.Tanh,
                                                               scale=vec05[:, 0, c:c + 1], bias=vec05[:, 1, c:c + 1]),
                  R=[y_r[c], const_r], W=[t2_r])
            P.dve(lambda e, yc=yc, c=c, vp_t=vp_t: e.tensor_scalar(
                out=vp_t[:], in0=yc, scalar1=vec05[:, 0, c:c + 1], scalar2=vec05[:, 1, c:c + 1],
                op0=ALU.mult, op1=ALU.add), R=[y_r[c], const_r], W=[vp_r])
            P.dve(lambda e, c=c, t2_t=t2_t, vp_t=vp_t: e.scalar_tensor_tensor(
                out=mixT[:, 8 + c, :], in0=t2_t[:], scalar=1.0, in1=vp_t[:], op0=ALU.add, op1=ALU.mult),
                R=[t2_r, vp_r], W=[mix_r[8 + c]])
            act_gate(pg, gth, ghg)
            P.dve(lambda e, c=c: e.scalar_tensor_tensor(
                out=sga[:, c, :], in0=gth[0][:], scalar=1.0, in1=ghg[0][:], op0=ALU.add, op1=ALU.mult),
                R=[gth[1], ghg[1]], W=[sga_r[c]])
        for co in range(8):
            pp_ = next_pair()
            for half in range(2):
                for ci in range(8):
                    P.pe(lambda e, half=half, ci=ci, co=co, pp_=pp_: e.matmul(
                        PP[pp_][:, half * 512:half * 512 + 512], lhsT=wpw[:, ci, co * 128:(co + 1) * 128],
                        rhs=mixT[:, 8 + ci, half * 512:half * 512 + 512], start=(ci == 0), stop=(ci == 7)),
                        R=[wpw_r, mix_r[8 + ci]], W=[preg[2 * pp_ + half]])
            P.dve(lambda e, co=co, pp_=pp_: e.scalar_tensor_tensor(
                out=mixT[:, co, :], in0=PP[pp_][:], scalar=vec[:, 3, co:co + 1], in1=sga[:, co, :],
                op0=ALU.add, op1=ALU.mult), R=pr(pp_) + [sga_r[co], const_r], W=[mix_r[co]])

    runB = {}

    def stage_attn(blk):
        e0 = blk * TB
        A.reset()
        NVT = 53
        qTn = [A.arena(f"qTn{i}", [128, TB], BF16) for i in range(2)]
        q4b = [A.arena(f"q4b{i}", [128, 4, 256], BF16) for i in range(2)]
        q16b = [A.arena(f"q16b{i}", [128, 17, 64], BF16) for i in range(2)]
        sgbb = [A.arena(f"sgb{i}", [128, TB], BF16) for i in range(3)]
        runB["start"] = A.mark()
        qsq, qsq_r = A.arena("qsq", [128, 512], BF16)
        qraw, qraw_r = A.arena("qraw", [128, 512], F32)
        qln, qln_r = A.arena("qln", [128, 512], F32)
        bth, bth_r = A.arena("bth", [128, 512], F32)
        bhg, bhg_r = A.arena("bhg", [128, 512], F32)
        kTw = [A.arena(f"kTw{i}", [128, 3 * TB], BF16) for i in range(2)]
        vt = [A.arena(f"vt{i}", [128, NVT, 2, 65], BF16) for i in range(2)]
        pt = [A.arena(f"pt{i}", [128, 1024], BF16) for i in range(3)]
        runB["end"] = A.mark()
        acc = [A.arena(f"acc{i}", [65, TB], F32) for i in range(2)]
        dnos = [A.arena(f"dno{i}", [128, TB], F32) for i in range(2)]
        lo = e0 - 2 * TB
        vsR = [vs_r[b] for b in range(max(blk - 2, 0), blk + 1)]
        P.fence()
        BQ, BN = 6, 7

        vparts = [[Reg(f"vt{i}_{k}", arena=True) for k in range(7)] for i in range(2)]

        def load_kv(hp):
            i = hp % 2
            (k_t, k_r), (v_t, _v) = kTw[i], vt[i]
            vr = vparts[i]
            P.dma("sp", k_t[:], kT_s[hp, :, lo:e0 + TB], R=[kTs_r[hp][b] for b in range(blk - 2, blk + 1)], W=[k_r])
            hs = slice(2 * hp, 2 * hp + 2)
            P.dma("sp", v_t[:, 0:9], v_s[e0 - 128:e0 + TB, hs, :].rearrange("(t p) h e -> p t h e", p=128),
                  R=vsR, W=[vr[0]])
            for r in range(4):
                P.dma("sp", v_t[:, 9 + 3 * r:12 + 3 * r],
                      v_s[sl(e0 - 512 + r, 384, 4), hs, :].rearrange("(m p) h e -> p m h e", p=128), R=vsR, W=[vr[1 + r]])
            P.dma("sp", v_t[:, 21:37], v_s[lo:e0, hs, :].rearrange("(p r) h e -> p r h e", r=16), R=vsR, W=[vr[5]])
            P.dma("sp", v_t[:, 37:53], v_s[e0 - TB:e0 + TB, hs, :].rearrange("(p r) h e -> p r h e", r=16),
                  R=vsR, W=[vr[6]])

        wts = {}

        def get_w(kind, hp):
            key = (kind, hp // 4)
            if key not in wts:
                c0 = COL[kind] + (hp // 4) * 512
                wts[key] = load_w(w_in[:, c0:c0 + 512])
            return wts[key]

        def prep_gen(hp):
            i = hp % 2
            j = hp % 4
            (qn_t, qn_r), (q4_t, q4_r), (q16_t, q16_r), (sg_t, sg_r) = qTn[i], q4b[i], q16b[i], sgbb[hp % 3]

            half_box = [0]

            def mm4(kind, half, c0):
                wb, w_r = get_w(kind, hp)
                for c in range(c0, c0 + 4):
                    P.pe(lambda e, c=c: e.matmul(bank_ap(BQ), lhsT=wb[:, c, j * 128:(j + 1) * 128],
                                                 rhs=hT[:, c, half * 512:half * 512 + 512],
                                                 start=(c == 0), stop=(c == 15)),
                         R=[w_r] + hT_r[4 * half:4 * half + 4], W=[preg[BQ]])

            def q_evac():
                P.act(lambda e: e.activation(out=qsq[:], in_=bank_ap(BQ), func=AF.Square), R=[preg[BQ]], W=[qsq_r])
                P.act(lambda e: e.copy(out=qraw[:], in_=bank_ap(BQ)), R=[preg[BQ]], W=[qraw_r])

            def q_norm(half):
                P.pe(lambda e: e.matmul(bank_ap(BN), lhsT=onesbd[:], rhs=qsq[:], start=True, stop=True),
                     R=[qsq_r, const_r], W=[preg[BN]])
                P.act(lambda e: e.activation(out=qln[:], in_=bank_ap(BN), func=AF.Ln, bias=RMS_EPS, scale=1.0 / 64),
                      R=[preg[BN]], W=[qln_r])
                P.act(lambda e: e.activation(out=qln[:], in_=qln[:], func=AF.Exp, scale=-0.5), R=[qln_r], W=[qln_r])
                P.dve(lambda e: e.scalar_tensor_tensor(out=qn_t[:, half * 512:half * 512 + 512], in0=qraw[:],
                                                       scalar=qkg[:, 0:1], in1=qln[:], op0=ALU.mult, op1=ALU.mult),
                      R=[qraw_r, qln_r, const_r], W=[qn_r])

            def g_evac():
                P.act(lambda e: e.activation(out=bth[:], in_=bank_ap(BQ), func=AF.Exp, scale=-1.0),
                      R=[preg[BQ]], W=[bth_r])
                P.act(lambda e: e.activation(out=bth[:], in_=bth[:], func=AF.Ln, bias=1.0), R=[bth_r], W=[bth_r])
                P.act(lambda e: e.activation(out=bth[:], in_=bth[:], func=AF.Exp, scale=-1.0), R=[bth_r], W=[bth_r])
                P.dve(lambda e, half=half_box[0]: e.tensor_tensor(out=sg_t[:, half * 512:half * 512 + 512],
                                                                 in0=bank_ap(BQ), in1=bth[:], op=ALU.mult),
                      R=[preg[BQ], bth_r], W=[sg_r])

            def g_comb(half):
                pass

            def perm_atoms():
                at = []
                for r in range(4):
                    at.append(lambda r=r: P.act(lambda e: e.copy(
                        out=q4_t[:, r, :], in_=qn_t[:].rearrange("p (j r) -> p r j", r=4)[:, r, :]),
                        R=[qn_r], W=[q4_r]))
                for g in range(4):
                    at.append(lambda g=g: P.act(lambda e: e.copy(
                        out=q16_t[:, 4 * g:4 * g + 4, :],
                        in_=qn_t[:].rearrange("p (j r) -> p r j", r=16)[:, 4 * g:4 * g + 4, :]),
                        R=[qn_r], W=[q16_r]))
                at.append(lambda: P.act(lambda e: e.copy(
                    out=q16_t[:, 16:17, :], in_=qn_t[:, 0:64].rearrange("p (r j) -> p r j", r=1)),
                    R=[qn_r], W=[q16_r]))
                return at

            def mm2(kind, half, c0):
                wb, w_r = get_w(kind, hp)
                for c in range(c0, c0 + 2):
                    P.pe(lambda e, c=c: e.matmul(bank_ap(BQ), lhsT=wb[:, c, j * 128:(j + 1) * 128],
                                                 rhs=hT[:, c, half * 512:half * 512 + 512],
                                                 start=(c == 0), stop=(c == 15)),
                         R=[w_r] + hT_r[4 * half:4 * half + 4], W=[preg[BQ]])

            for half in range(2):
                hs_ = slice(half * 512, half * 512 + 512)
                for c0 in range(0, 16, 2):
                    mm2("q", half, c0)
                    yield
                P.act(lambda e: e.activation(out=qsq[:], in_=bank_ap(BQ), func=AF.Square), R=[preg[BQ]], W=[qsq_r])
                yield
                P.act(lambda e: e.copy(out=qraw[:], in_=bank_ap(BQ)), R=[preg[BQ]], W=[qraw_r])
                yield
                side = [
                    lambda: P.pe(lambda e: e.matmul(bank_ap(BN), lhsT=onesbd[:], rhs=qsq[:], start=True, stop=True),
                                 R=[qsq_r, const_r], W=[preg[BN]]),
                    lambda: P.act(lambda e: e.activation(out=qln[:], in_=bank_ap(BN), func=AF.Ln, bias=RMS_EPS,
                                                         scale=1.0 / 64), R=[preg[BN]], W=[qln_r]),
                    lambda: P.act(lambda e: e.activation(out=qln[:], in_=qln[:], func=AF.Exp, scale=-0.5),
                                  R=[qln_r], W=[qln_r]),
                    lambda hs_=hs_: P.dve(lambda e: e.scalar_tensor_tensor(
                        out=qn_t[:, hs_], in0=qraw[:], scalar=qkg[:, 0:1], in1=qln[:], op0=ALU.mult, op1=ALU.mult),
                        R=[qraw_r, qln_r, const_r], W=[qn_r]),
                ]
                pat = perm_atoms() if half == 1 else []
                side += pat[:4]
                for n_, c0 in enumerate(range(0, 16, 2)):
                    mm2("bg", half, c0)
                    yield
                    if n_ < len(side) and side[n_] is not None:
                        side[n_]()
                        yield
                P.act(lambda e: e.activation(out=bth[:], in_=bank_ap(BQ), func=AF.Exp, scale=-1.0),
                      R=[preg[BQ]], W=[bth_r])
                yield
                P.act(lambda e: e.activation(out=bth[:], in_=bth[:], func=AF.Ln, bias=1.0), R=[bth_r], W=[bth_r])
                yield
                P.act(lambda e: e.activation(out=bth[:], in_=bth[:], func=AF.Exp, scale=-1.0), R=[bth_r], W=[bth_r])
                yield
                P.dve(lambda e, hs_=hs_: e.tensor_tensor(out=sg_t[:, hs_], in0=bank_ap(BQ), in1=bth[:], op=ALU.mult),
                      R=[preg[BQ], bth_r], W=[sg_r])
                yield
                for atom in pat[4:]:
                    atom()
                    yield

        def run_all(gen):
            for _ in gen:
                pass

        def sbank():
            ctr["sb"] += 1
            return ctr["sb"] % 4

        def obank():
            ctr["ob"] += 1
            return 4 + ctr["ob"] % 2

        groups = []

        def add_call(hp, h, units, QN, mask, evac, half=False):
            per_s = 512 // (2 * QN)
            call = dict(ob=None)
            ng = len(units) // per_s
            for gi in range(ng):
                groups.append(dict(hp=hp, h=h, QN=QN, mask=mask, call=call, g0=gi * per_s, half=half,
                                   units=units[gi * per_s:(gi + 1) * per_s], evac=evac if gi == ng - 1 else None,
                                   fin=None))

        for hp in range(8):
            for h in range(2):
                a_t, a_r = acc[h]
                for half in range(2):
                    units = [(("n", qt * 128), TB * 2 - 128 + qt * 128, TB * 2 + qt * 128, 1, qt, qt + 1)
                             for qt in range(4 * half, 4 * half + 4)]
                    add_call(hp, h, units, 128, mask128,
                             lambda ob, half=half, a_t=a_t, a_r=a_r: P.act(
                                 lambda e: e.copy(out=a_t[:, half * 512:half * 512 + 512], in_=bank_ap(ob)[0:65, :]),
                                 R=[preg[ob]], W=[a_r]))
                for m in range(2):
                    units = [(("4", r * 256 + 128 * m), 2 * TB + r + 512 * (m - 1), 2 * TB + r + 512 * m, 4,
                              9 + 3 * r + m, 9 + 3 * r + m + 1) for r in range(4)]

                    def ev4(ob, m=m, a_t=a_t, a_r=a_r):
                        av = a_t[:, m * 512:m * 512 + 512].rearrange("p (j r) -> p r j", r=4)
                        P.dve(lambda e: e.tensor_tensor(
                            out=av, in0=bank_ap(ob)[0:65, :].rearrange("p (r j) -> p r j", r=4), in1=av, op=ALU.add),
                            R=[preg[ob], a_r], W=[a_r])
                    add_call(hp, h, units, 128, mask128, ev4)
                for g in range(4):
                    units = [(("16", r * 64), r, TB + r, 16, 21 + r, 37 + r) for r in range(4 * g, 4 * g + 4)]

                    def ev16(ob, g=g, a_t=a_t, a_r=a_r):
                        av = a_t[:, :].rearrange("p (j r) -> p r j", r=16)[:, 4 * g:4 * g + 4, :]
                        src = bank_ap(ob)[0:65, :].rearrange("p (r jj) -> p r jj", r=4)[:, :, 0:64]
                        P.dve(lambda e: e.tensor_tensor(out=av, in0=src, in1=av, op=ALU.add),
                              R=[preg[ob], a_r], W=[a_r])
                    add_call(hp, h, units, 128, mask16, ev16, half=True)
                groups[-1]["fin"] = (hp, h)
        for gi_, G_ in enumerate(groups):
            G_["gi"] = gi_

        def finish_head_dma(hp, h):
            a_t, a_r = acc[h]
            ctr["ds"] += 1
            di = ctr["ds"] % 4
            dno, dno_r = dnos[h]
            P.dma("sp", d_s[di], a_t[64:65, :], R=[a_r], W=[ds_r[di]])
            P.dma("sp", dno[0:64, :], d_s[di].partition_broadcast(64), R=[ds_r[di]], W=[dno_r])

        def finish_head_atoms(hp, h):
            a_t, a_r = acc[h]
            hb = 64 * h
            sg_t, sg_r = sgbb[hp % 3]
            dno, dno_r = dnos[h]
            atoms = []
            for hf in range(2):
                cs = slice(hf * 512, hf * 512 + 512)
                atoms.append(lambda cs=cs: P.act(lambda e: e.activation(out=dno[0:64, cs], in_=dno[0:64, cs], func=AF.Ln),
                                                 R=[dno_r], W=[dno_r]))
                atoms.append(lambda cs=cs: P.act(lambda e: e.activation(out=dno[0:64, cs], in_=dno[0:64, cs], func=AF.Exp,
                                                                        scale=-1.0), R=[dno_r], W=[dno_r]))
            for hf in range(2):
                cs = slice(hf * 512, hf * 512 + 512)
                atoms.append(lambda cs=cs: P.dve(lambda e: e.tensor_tensor(
                    out=dno[hb:hb + 64, cs], in0=a_t[0:64, cs], in1=dno[0:64, cs], op=ALU.mult),
                    R=[a_r, dno_r], W=[dno_r]))
                atoms.append(lambda cs=cs: P.dve(lambda e: e.tensor_tensor(
                    out=mixT[hb:hb + 64, 8 + hp, cs], in0=dno[hb:hb + 64, cs], in1=sg_t[hb:hb + 64, cs], op=ALU.mult),
                    R=[dno_r, sg_r], W=[mix_r[8 + hp]]))
            return atoms

        def emit_S(G):
            hp, h, QN = G["hp"], G["h"], G["QN"]
            (k_t, k_r) = kTw[hp % 2]
            hb = 64 * h
            sb = groups.index(G) % 4 if "gi" not in G else G["gi"] % 4
            G["sb"] = sb
            i = hp % 2
            for uu, ((qlay, q0), kA0, kB0, ks, vA, vB) in enumerate(G["units"]):
                base = uu * 2 * QN
                if qlay == "n":
                    rhs, q_r = qTn[i][0][hb:hb + 64, q0:q0 + QN], qTn[i][1]
                elif qlay == "4":
                    rhs, q_r = q4b[i][0][hb:hb + 64].rearrange("p r j -> p (r j)")[:, q0:q0 + QN], q4b[i][1]
                else:
                    rhs, q_r = q16b[i][0][hb:hb + 64].rearrange("p r j -> p (r j)")[:, q0:q0 + QN], q16b[i][1]
                P.pe(lambda e, base=base, rhs=rhs, kA0=kA0, ks=ks: e.matmul(
                    bank_ap(sb)[:, base:base + QN], lhsT=k_t[hb:hb + 64, sl(kA0, 128, ks)], rhs=rhs,
                    start=True, stop=True), R=[k_r, q_r], W=[preg[sb]])
                P.pe(lambda e, base=base, rhs=rhs, kB0=kB0, ks=ks: e.matmul(
                    bank_ap(sb)[:, base + QN:base + 2 * QN], lhsT=k_t[hb:hb + 64, sl(kB0, 128, ks)], rhs=rhs,
                    start=True, stop=True), R=[k_r, q_r], W=[preg[sb]])

        def emit_expmask2(GA, GB):
            sbA, sbB = GA["sb"], GB["sb"]
            assert sbB == sbA + 1 and sbA % 2 == 0 and GA["mask"] is GB["mask"] and GA["half"] == GB["half"]
            mask = GA["mask"]
            ctr["pt"] = ctr.get("pt", 0) + 1
            p_t, p_r = pt[ctr["pt"] % 3]
            GA["pt"] = (p_t[:, 0:512], p_r)
            GB["pt"] = (p_t[:, 512:1024], p_r)
            src = PP[sbA // 2][:, :]
            m2 = mask[:, :].unsqueeze(1).to_broadcast([128, 2, 512])
            if GA["half"]:
                v = lambda ap: ap.rearrange("p (u j) -> p u j", j=128)[:, :, 0:64]
                mv = m2.rearrange("p t (u j) -> p t u j", j=128)[:, :, :, 0:64]
                pv = lambda ap: ap.rearrange("p (t u j) -> p t u j", t=2, j=128)[:, :, :, 0:64]
            else:
                v = lambda ap: ap
                mv = m2
                pv = lambda ap: ap.rearrange("p (t c) -> p t c", t=2)
            P.act(lambda e: e.activation(out=v(p_t[:]), in_=v(src), func=AF.Exp, scale=0.125),
                  R=[preg[sbA], preg[sbB]], W=[p_r])
            P.dve(lambda e: e.tensor_tensor(out=pv(p_t[:]), in0=pv(p_t[:]), in1=mv, op=ALU.mult),
                  R=[p_r, const_r], W=[p_r])

        def emit_PV(G):
            hp, h, QN = G["hp"], G["h"], G["QN"]
            (v_t, _v) = vt[hp % 2]
            vr = vparts[hp % 2]
            call = G["call"]
            if call["ob"] is None:
                call["ob"] = obank()
            ob = call["ob"]
            p_t, p_r = G["pt"]
            for uu, (_q, kA0, kB0, ks, vA, vB) in enumerate(G["units"]):
                base = uu * 2 * QN
                oc = (G["g0"] + uu) * QN
                P.pe(lambda e, oc=oc, base=base, vA=vA: e.matmul(
                    bank_ap(ob)[0:65, oc:oc + QN], lhsT=v_t[:, vA, h, :], rhs=p_t[:, base:base + QN],
                    start=True, stop=False), R=vr + [p_r], W=[preg[ob]])
                P.pe(lambda e, oc=oc, base=base, vB=vB: e.matmul(
                    bank_ap(ob)[0:65, oc:oc + QN], lhsT=v_t[:, vB, h, :], rhs=p_t[:, base + QN:base + 2 * QN],
                    start=False, stop=True), R=vr + [p_r], W=[preg[ob]])

        deferred = []

        def emit_evac(G, it):
            if G["evac"] is not None:
                G["evac"](G["call"]["ob"])
            if G["fin"] is not None:
                fhp, fh = G["fin"]
                finish_head_dma(fhp, fh)
                for k_, atom in enumerate(finish_head_atoms(fhp, fh)):
                    deferred.append((it + 5 + k_ // 2, atom))
                if fh == 1 and fhp + 2 < 8:
                    load_kv(fhp + 2)

        load_kv(0)
        load_kv(1)
        run_all(prep_gen(0))
        n = len(groups)
        GP = 2
        nit = (n + GP - 1) // GP
        IT_HP = 32 // GP
        gen = None
        for it in range(nit + 3):
            if it % IT_HP == 0 and it < nit:
                if gen is not None:
                    run_all(gen)
                nxt = it // IT_HP + 1
                gen = prep_gen(nxt) if nxt < 8 else None
            odd = it % 2 == 1
            if not odd:
                for gi in range(it * GP, it * GP + GP):
                    if 0 <= gi < n:
                        emit_S(groups[gi])
            if 0 <= (it - 1) * GP < n:
                emit_expmask2(groups[(it - 1) * GP], groups[(it - 1) * GP + 1])
            for gi in range((it - 2) * GP, (it - 2) * GP + GP):
                if 0 <= gi < n:
                    emit_PV(groups[gi])
            for gi in range((it - 3) * GP, (it - 3) * GP + GP):
                if 0 <= gi < n:
                    emit_evac(groups[gi], it)
            if gen is not None:
                for _ in range(4):
                    next(gen, None)
            while deferred and deferred[0][0] <= it:
                deferred.pop(0)[1]()
            if odd:
                for gi in range(it * GP, it * GP + GP):
                    if 0 <= gi < n:
                        emit_S(groups[gi])
        for _, fn in deferred:
            fn()
        runB["regs"] = ([qsq_r, qraw_r, qln_r, bth_r, bhg_r] + [r_ for _, r_ in kTw] + [r_ for _, r_ in vt]
                        + [r_ for part in vparts for r_ in part] + [r_ for _, r_ in pt])

    def stage_out(blk, nxt=None):
        A.reset(runB["start"])
        ot = [A.arena(f"ot{i}", [128, 512], F32) for i in range(2)]
        xr = [A.arena(f"xr{i}", [128, 512], F32) for i in range(4)]
        b0 = s0_alloc() if nxt is not None else None
        assert A.mark() <= runB["end"], (A.mark(), runB["end"])
        P.fence_regs(runB["regs"])
        bg = s0_gen(nxt, hT, hT_r, b0) if nxt is not None else None
        PS["split"] = bg is not None
        n = 0

        def ld(cb, tt, n):
            x_t, x_r = xr[n % 4]
            row0 = blk * TB + tt * 128
            P.dma("sp", x_t[:], xe[row0:row0 + 128, cb * 512:cb * 512 + 512], W=[x_r])
        for n0 in range(3):
            ld(n0 // 8, n0 % 8, n0)
        for cb in range(4):
            wb, w_r = load_w(w_out[:, cb * 512:cb * 512 + 512])
            for tt in range(8):
                nn = n + 3
                if nn < 32:
                    ld(nn // 8, nn % 8, nn)
                b = next_bank()
                for c in range(16):
                    P.pe(lambda e, b=b, c=c, tt=tt, wb=wb: e.matmul(
                        bank_ap(b), lhsT=mixT[:, c, tt * 128:(tt + 1) * 128], rhs=wb[:, c, :],
                        start=(c == 0), stop=(c == 15)), R=[w_r, mix_r[c]], W=[preg[b]])
                x_t, x_r = xr[n % 4]
                o_t, o_r = ot[n % 2]
                P.dve(lambda e, b=b, x_t=x_t, o_t=o_t: e.tensor_tensor(out=o_t[:], in0=bank_ap(b), in1=x_t[:], op=ALU.add),
                      R=[preg[b], x_r], W=[o_r])
                row0 = (blk - 2) * TB + tt * 128
                out_ops.append(P.dma("sp", out_d[row0:row0 + 128, cb * 512:cb * 512 + 512], o_t[:], R=[o_r]))
                n += 1
                if bg is not None and (n == 1 or (n >= 16 and n % 2 == 0)):
                    next(bg, None)
        if bg is not None:
            run_all(bg)
        PS["split"] = False

    HA, HB = (hT, hT_r), (hTB, hTB_r)
    A.reset()
    b0 = s0_alloc()
    P.fence()
    g0 = s0_gen(0, *HA, b0, deep=True)
    next(g0)
    load_consts()
    run_all(g0)
    finish_consts()
    A.reset()
    kvb, chb, b0 = kv_alloc(), ch_alloc(), s0_alloc()
    P.fence()
    H.update(t=HA[0], r=HA[1])
    PS["split"] = True
    g = s0_gen(1, *HB, b0)
    stage_kv(0, kvb, bg=g)
    run_all(g)
    H.update(t=HB[0], r=HB[1])
    g = s0_gen(2, *HA, b0)
    stage_kv(1, kvb, bg=g)
    stage_conv_halo(chb, bgen=g)
    run_all(g)
    PS["split"] = False
    H.update(t=HA[0], r=HA[1])
    for blk in (2, 3):
        A.reset()
        kvb = kv_alloc()
        P.fence()
        stage_kv(blk, kvb)
        stage_conv(blk)
        stage_attn(blk)
        stage_out(blk, nxt=3 if blk == 2 else None)
    P.emit(final_wait_ops=out_ops)
    return nc


def _consts():
    bf = ml_dtypes.bfloat16
    a = np.arange(128)[:, None]
    j = np.arange(128)[None, :]
    mA = (a >= j).astype(np.float32)
    mB = (a <= j).astype(np.float32)
    m128 = np.concatenate([mA, mB, mA, mB], axis=1)
    jj = np.arange(128)[None, :]
    real = jj < 64
    mA16 = ((a >= jj) & real).astype(np.float32)
    mB16 = ((a >= 64) & ((a - 64) <= jj) & real).astype(np.float32)
    m16 = np.concatenate([mA16, mB16, mA16, mB16], axis=1)
    onesbd = np.zeros((128, 128), np.float32)
    onesbd[:64, :64] = 1
    onesbd[64:, 64:] = 1
    return dict(mask128=m128.astype(bf), mask16=m16.astype(bf), ident=np.eye(128, dtype=np.float32).astype(bf),
                onesbd=onesbd.astype(bf), onesf=np.ones((128, 128), np.float32).astype(bf))


def kernel(x, norm_g, w_in, conv_w, conv_b, conv_norm_g, conv_norm_b, conv_pw_w, conv_pw_b,
           q_norm_g, k_norm_g, w_out):
    f = lambda a: np.ascontiguousarray(np.asarray(a, dtype=np.float32))
    x = f(x)
    cm = lambda v: np.ascontiguousarray(f(v)[0].reshape(8, 128).T)
    shared = dict(
        gb=np.ascontiguousarray(np.broadcast_to(f(norm_g)[0], (128, D))),
        w_in=f(w_in)[0],
        convw=np.ascontiguousarray(f(conv_w)[0].reshape(31, 8, 128).transpose(2, 1, 0)),
        vec=np.ascontiguousarray(np.stack([cm(conv_b), cm(conv_norm_g), cm(conv_norm_b), cm(conv_pw_b)], axis=1)),
        qkg=np.ascontiguousarray(np.stack([np.tile(f(q_norm_g)[0], 2), np.tile(f(k_norm_g)[0], 2)], axis=1)),
        pw=f(conv_pw_w)[0],
        w_out=f(w_out)[0],
        **_consts(),
    )
    in_maps = []
    for c in range(8):
        b, half = c // 2, c % 2
        main = x[b, half * TPC:(half + 1) * TPC]
        halo = x[b, 0:TPC] if half == 1 else np.zeros((TPC, D), np.float32)
        m = dict(shared)
        m["xe"] = np.ascontiguousarray(np.concatenate([halo, main], axis=0))
        m["flag"] = np.full((128, 1), float(half), np.float32)
        in_maps.append(m)
    nc = build_program()
    res = run_bass_kernel_spmd(nc, in_maps, core_ids=list(range(8)))
    out = np.empty((NB, S, D), np.float32)
    for c in range(8):
        b, half = c // 2, c % 2
        out[b, half * TPC:(half + 1) * TPC] = res.results[c]["out"]
    return out
```
